# Optimizing a Trainium2 kernel written in Bass

```python
import jax
import jax.numpy as jnp
from jax import lax
import numpy as np

D_MODEL = 1024
BATCH = 8
SEQ = 4096
DEPTH = 2

HEAD_DIM = 64
H_NA = 4
H_DIL = 6
H_GQ = 6
H_GKV = 2
NA_W = H_NA * HEAD_DIM
DIL_W = H_DIL * HEAD_DIM
GQ_W = H_GQ * HEAD_DIM
GKV_W = H_GKV * HEAD_DIM
D_MIX = NA_W + DIL_W + GQ_W
D_IN = 3 * NA_W + 3 * DIL_W + GQ_W + 2 * GKV_W

GRID_W = 64
NA_KH_MAX = 8
NA_KW = 16
DIL_PATTERNS = ((128, 1), (512, 4), (2048, 16))
ROPE_THETA = 500000.0
ROPE_DIMS = HEAD_DIM // 4
AXIAL_THETA = 10000.0
Q_BLK = 128
D_FF = 2816
CONV_W = 3
EPS = 1e-6
NEG_INF = -1e30

kernel_name = 'hybrid_parallel_heads_encoder'


def rms_norm(x, g):
    xf = x.astype(jnp.float32)
    y = xf * lax.rsqrt(jnp.mean(xf * xf, axis=-1, keepdims=True) + EPS)
    return (y * g.astype(jnp.float32)).astype(x.dtype)


def rope_cos_sin(pos, dim, theta):
    inv = theta ** (-jnp.arange(0, dim, 2, dtype=jnp.float32) / dim)
    ang = pos.astype(jnp.float32)[:, None] * inv[None, :]
    return jnp.cos(ang), jnp.sin(ang)


def rotate(x, cos, sin):
    half = x.shape[-1] // 2
    x1 = x[..., :half].astype(jnp.float32)
    x2 = x[..., half:].astype(jnp.float32)
    c = cos[None, :, None, :]
    s = sin[None, :, None, :]
    return jnp.concatenate([x1 * c - x2 * s, x2 * c + x1 * s], axis=-1).astype(x.dtype)


def partial_rotary(x, cos, sin):
    return jnp.concatenate([rotate(x[..., :ROPE_DIMS], cos, sin), x[..., ROPE_DIMS:]], axis=-1)


def axial_rotary(x, cos_r, sin_r, cos_c, sin_c):
    half = x.shape[-1] // 2
    return jnp.concatenate([rotate(x[..., :half], cos_r, sin_r), rotate(x[..., half:], cos_c, sin_c)], axis=-1)


def neighbourhood_tables(S):
    rows = S // GRID_W
    kh = min(NA_KH_MAX, rows)
    t = jnp.arange(S)
    r = t // GRID_W
    c = t % GRID_W
    r0 = jnp.clip(r - kh // 2, 0, rows - kh)
    c0 = jnp.clip(c - NA_KW // 2, 0, GRID_W - NA_KW)
    kr = r0[:, None] + jnp.arange(kh)[None, :]
    kc = c0[:, None] + jnp.arange(NA_KW)[None, :]
    idx = (kr[:, :, None] * GRID_W + kc[:, None, :]).reshape(S, kh * NA_KW)
    dr = (kr - r[:, None] + NA_KH_MAX - 1)[:, :, None]
    dc = (kc - c[:, None] + NA_KW - 1)[:, None, :]
    return idx, dr, dc


def neighbourhood_attention(q, k, v, rpb, idx, dr, dc):
    B, S, H, hd = q.shape
    nk = idx.shape[1]
    nb = S // Q_BLK
    scale = hd ** -0.5
    bias = rpb[:, dr, dc].reshape(H, S, nk).astype(jnp.float32)
    qb = q.reshape(B, nb, Q_BLK, H, hd).transpose(1, 0, 2, 3, 4)
    idxb = idx.reshape(nb, Q_BLK, nk)
    biasb = bias.reshape(H, nb, Q_BLK, nk).transpose(1, 0, 2, 3)

    def block(args):
        qi, ii, bi = args
        kg = k[:, ii]
        vg = v[:, ii]
        s = jnp.einsum('bqhd,bqkhd->bhqk', qi, kg).astype(jnp.float32) * scale + bi[None]
        p = jax.nn.softmax(s, axis=-1).astype(v.dtype)
        return jnp.einsum('bhqk,bqkhd->bqhd', p, vg)

    o = lax.map(block, (qb, idxb, biasb))
    return o.transpose(1, 0, 2, 3, 4).reshape(B, S, H * hd)


def dilated_tables(S):
    t = jnp.arange(S)
    idxs, valids = [], []
    for w, d in DIL_PATTERNS:
        n = (w // 2) // d
        pos = t[:, None] + (jnp.arange(-n, n + 1) * d)[None, :]
        valids.append((pos >= 0) & (pos < S))
        idxs.append(jnp.clip(pos, 0, S - 1))
    return tuple(idxs), tuple(valids)


def dilated_window_attention(q, k, v, idxs, valids):
    B, S, H, hd = q.shape
    nb = S // Q_BLK
    scale = hd ** -0.5
    qb = q.reshape(B, nb, Q_BLK, H, hd).transpose(1, 0, 2, 3, 4)
    idxb = tuple(i.reshape(nb, Q_BLK, -1) for i in idxs)
    valb = tuple(m.reshape(nb, Q_BLK, -1) for m in valids)

    def block(args):
        qi, ii, mm = args
        lses, outs = [], []
        for ig, mg in zip(ii, mm):
            kg = k[:, ig]
            vg = v[:, ig]
            s = jnp.einsum('bqhd,bqkhd->bhqk', qi, kg).astype(jnp.float32) * scale
            s = jnp.where(mg[None, None], s, NEG_INF)
            lse = jax.nn.logsumexp(s, axis=-1)
            p = jnp.exp(s - lse[..., None]).astype(v.dtype)
            outs.append(jnp.einsum('bhqk,bqkhd->bqhd', p, vg))
            lses.append(lse)
        wts = jax.nn.softmax(jnp.stack(lses), axis=0).astype(v.dtype)
        return jnp.einsum('gbhq,gbqhd->bqhd', wts, jnp.stack(outs))

    o = lax.map(block, (qb, idxb, valb))
    return o.transpose(1, 0, 2, 3, 4).reshape(B, S, H * hd)


def gqa_block_attention(q, k, v):
    B, S, Hq, hd = q.shape
    Hkv = k.shape[2]
    g = Hq // Hkv
    nb = S // Q_BLK
    scale = hd ** -0.5
    qb = q.reshape(B, nb, Q_BLK, Hkv, g, hd).transpose(1, 0, 2, 3, 4, 5)

    def block(qi):
        s = jnp.einsum('bqkgd,bskd->bkgqs', qi, k).astype(jnp.float32) * scale
        p = jax.nn.softmax(s, axis=-1).astype(v.dtype)
        return jnp.einsum('bkgqs,bskd->bqkgd', p, v)

    o = lax.map(block, qb)
    return o.transpose(1, 0, 2, 3, 4, 5).reshape(B, S, Hq * hd)


def depthwise_conv_centred(z, w, b):
    S = z.shape[1]
    pad = CONV_W // 2
    zp = jnp.pad(z, ((0, 0), (pad, pad), (0, 0)))
    out = zp[:, 0:S] * w[0] + b
    for j in range(1, CONV_W):
        out = out + zp[:, j:j + S] * w[j]
    return out


def mixer_sublayer(x, n1, w_in_l, qg, kg, rpb_l, og, w_out_l, tables):
    (na_idx, na_dr, na_dc, dil_idx, dil_val, cos1, sin1, cos_r, sin_r, cos_c, sin_c) = tables
    B, S, _ = x.shape
    h = rms_norm(x, n1)
    proj = h @ w_in_l
    sizes = (NA_W, NA_W, NA_W, DIL_W, DIL_W, DIL_W, GQ_W, GKV_W, GKV_W)
    cuts = [int(c) for c in np.cumsum(sizes)[:-1]]
    qa, ka, va, qd, kd, vd, qc, kc, vc = jnp.split(proj, cuts, axis=-1)

    def heads(z, n):
        return z.reshape(B, S, n, HEAD_DIM)

    qa = rms_norm(heads(qa, H_NA), qg[0])
    ka = rms_norm(heads(ka, H_NA), kg[0])
    out_a = neighbourhood_attention(qa, ka, heads(va, H_NA), rpb_l, na_idx, na_dr, na_dc)
    qd = partial_rotary(rms_norm(heads(qd, H_DIL), qg[1]), cos1, sin1)
    kd = partial_rotary(rms_norm(heads(kd, H_DIL), kg[1]), cos1, sin1)
    out_d = dilated_window_attention(qd, kd, heads(vd, H_DIL), dil_idx, dil_val)
    qc = axial_rotary(rms_norm(heads(qc, H_GQ), qg[2]), cos_r, sin_r, cos_c, sin_c)
    kc = axial_rotary(rms_norm(heads(kc, H_GKV), kg[2]), cos_r, sin_r, cos_c, sin_c)
    out_c = gqa_block_attention(qc, kc, heads(vc, H_GKV))

    mix = jnp.concatenate([
        rms_norm(out_a, og[:NA_W]),
        rms_norm(out_d, og[NA_W:NA_W + DIL_W]),
        rms_norm(out_c, og[NA_W + DIL_W:]),
    ], axis=-1)
    return x + mix @ w_out_l


def channel_sublayer(x, n2, w_gu, cw, cb, w_dn):
    h = rms_norm(x, n2)
    g, u = jnp.split(h @ w_gu, 2, axis=-1)
    g = depthwise_conv_centred(g, cw, cb)
    return x + (jax.nn.gelu(g, approximate=False) * u) @ w_dn


def setup_inputs(seed: int = 0) -> dict:
    key = jax.random.key(seed)
    ks = jax.random.split(key, 13)

    def nrm(k, shape, s):
        return s * jax.random.normal(k, shape, jnp.float32)

    return {
        'x': nrm(ks[0], (BATCH, SEQ, D_MODEL), 1.0),
        'norm1_g': 1.0 + nrm(ks[1], (DEPTH, D_MODEL), 0.02),
        'w_in': nrm(ks[2], (DEPTH, D_MODEL, D_IN), D_MODEL ** -0.5),
        'q_norm_g': 1.0 + nrm(ks[3], (DEPTH, 3, HEAD_DIM), 0.02),
        'k_norm_g': 1.0 + nrm(ks[4], (DEPTH, 3, HEAD_DIM), 0.02),
        'rpb': nrm(ks[5], (DEPTH, H_NA, 2 * NA_KH_MAX - 1, 2 * NA_KW - 1), 0.1),
        'out_norm_g': 1.0 + nrm(ks[6], (DEPTH, D_MIX), 0.02),
        'w_out': nrm(ks[7], (DEPTH, D_MIX, D_MODEL), D_MIX ** -0.5),
        'norm2_g': 1.0 + nrm(ks[8], (DEPTH, D_MODEL), 0.02),
        'w_gate_up': nrm(ks[9], (DEPTH, D_MODEL, 2 * D_FF), D_MODEL ** -0.5),
        'conv_w': nrm(ks[10], (DEPTH, CONV_W, D_FF), CONV_W ** -0.5),
        'conv_b': nrm(ks[11], (DEPTH, D_FF), 0.02),
        'w_down': nrm(ks[12], (DEPTH, D_FF, D_MODEL), D_FF ** -0.5),
    }


def reference(x, norm1_g, w_in, q_norm_g, k_norm_g, rpb, out_norm_g, w_out, norm2_g, w_gate_up, conv_w, conv_b, w_down):
    S = x.shape[1]
    t = jnp.arange(S, dtype=jnp.int32)
    cos1, sin1 = rope_cos_sin(t, ROPE_DIMS, ROPE_THETA)
    cos_r, sin_r = rope_cos_sin(t // GRID_W, HEAD_DIM // 2, AXIAL_THETA)
    cos_c, sin_c = rope_cos_sin(t % GRID_W, HEAD_DIM // 2, AXIAL_THETA)
    na_idx, na_dr, na_dc = neighbourhood_tables(S)
    dil_idx, dil_val = dilated_tables(S)
    tables = (na_idx, na_dr, na_dc, dil_idx, dil_val, cos1, sin1, cos_r, sin_r, cos_c, sin_c)
    for l in range(DEPTH):
        x = mixer_sublayer(x, norm1_g[l], w_in[l], q_norm_g[l], k_norm_g[l], rpb[l],
                           out_norm_g[l], w_out[l], tables)
        x = channel_sublayer(x, norm2_g[l], w_gate_up[l], conv_w[l], conv_b[l], w_down[l])
    return x
```

```python
import bisect
from contextlib import ExitStack
import numpy as np
import concourse.bass as bass
import concourse.mybir as mybir
from concourse.bass_utils import run_bass_kernel_spmd

F32 = mybir.dt.float32
BF16 = mybir.dt.bfloat16
AF = mybir.ActivationFunctionType
ALU = mybir.AluOpType
AX = mybir.AxisListType

S = 4096
D = 1024
NL = 2
DIN = 2560
DFF = 2816
NCH = DFF // 128
NT = S // 128
EPS = 1e-6
NEG = -30000.0
TB = 256
NB = S // TB


class V:
    def __init__(self, ap, t):
        self.ap = ap
        self.t = t

    def __getitem__(self, idx):
        return V(self.ap[idx], self.t)

    def rearrange(self, s, **kw):
        return V(self.ap.rearrange(s, **kw), self.t)

    def unsqueeze(self, a):
        return V(self.ap.unsqueeze(a), self.t)

    def to_broadcast(self, shp):
        return V(self.ap.to_broadcast(shp), self.t)

    def bitcast(self, dt):
        return V(self.ap.bitcast(dt), self.t)

    def partition_broadcast(self, n):
        return V(self.ap.partition_broadcast(n), self.t)


class Tile:
    def __init__(self, base, name, multi=False):
        self.base = base
        self.name = name
        self.w = {}
        self.r = {}
        self.dsem = None
        self.dcnt = 0
        self.multi = multi

    def __getitem__(self, idx):
        return V(self.base[idx], self)


class Eng:
    def __init__(self, name, eng, sem):
        self.name = name
        self.eng = eng
        self.sem = sem
        self.insts = []
        self.stamped = []
        self.seen = {}


class K:
    def __init__(self, nc):
        self.nc = nc
        self.E = {}
        for name, eng in (("pe", nc.tensor), ("act", nc.scalar), ("dve", nc.vector),
                          ("pool", nc.gpsimd), ("sp", nc.sync)):
            self.E[name] = Eng(name, eng, nc.alloc_semaphore(name="sem_" + name))
        self.dsems = {}
        self.dtiles = []
        self.free_slots = []
        self.scopes = []

    def sb(self, stack, name, shape, dt, multi=False):
        self.uid = getattr(self, "uid", 0) + 1
        name = "%s_u%d" % (name, self.uid)
        h = stack.enter_context(self.nc.sbuf_tensor(name, shape, dt))
        t = Tile(h, name, multi)
        if self.scopes:
            self.scopes[-1].append(t)
        return t

    def open_scope(self):
        self.scopes.append([])

    def close_scope(self):
        self.barrier()
        for t in self.scopes.pop():
            if t.dsem is not None:
                self.free_slots.append((t.dsem, t.dcnt))
                self.dtiles.remove(t)
                t.dsem = None

    def dram(self, name, shape, dt, kind="Internal"):
        h = self.nc.dram_tensor(name, shape, dt, kind=kind)
        return Tile(h.ap(), name, multi=True)

    def need(self, E, key, v):
        if key[0] == "e":
            P = self.E[key[1]]
            pos = bisect.bisect_left(P.stamped, v)
            if pos < len(P.stamped):
                val = pos + 1
            else:
                P.insts[v].then_inc(P.sem, 1)
                P.stamped.append(v)
                val = len(P.stamped)
            sem = P.sem
        else:
            sem = self.dsems[key[1]]
            val = v
        if E.seen.get(key, 0) >= val:
            return
        E.eng.wait_ge(sem, val)
        E.seen[key] = val

    def op(self, ename, reads, writes, fn):
        E = self.E[ename]
        me = ("e", ename)
        for t in reads:
            for key, v in list(t.w.items()):
                if key == me and ename == "pe":
                    continue
                self.need(E, key, v)
        for t in writes:
            for key, v in list(t.w.items()) + list(t.r.items()):
                if key == me and ename == "pe":
                    continue
                self.need(E, key, v)
        inst = fn(E.eng)
        idx = len(E.insts)
        E.insts.append(inst)
        for t in reads:
            t.r[me] = idx
        for t in writes:
            if t.multi:
                t.w[me] = idx
            else:
                t.w = {me: idx}
                t.r = {}
        return inst

    def dma(self, out, in_, q="sp"):
        E = self.E[q]
        for key, v in list(in_.t.w.items()):
            self.need(E, key, v)
        deps = list(out.t.r.items())
        if not out.t.multi:
            deps += list(out.t.w.items())
        for key, v in deps:
            self.need(E, key, v)
        st = out.t if not isinstance(out.t.base, bass.AP) else in_.t
        if st.dsem is None:
            if self.free_slots:
                st.dsem, st.dcnt = self.free_slots.pop()
            else:
                st.dsem = self.nc.alloc_semaphore(name="d%d" % len(self.dsems))
                self.dsems[st.dsem.num] = st.dsem
            self.dtiles.append(st)
        st.dcnt += 16
        E.eng.dma_start(out=out.ap, in_=in_.ap).then_inc(st.dsem, 16)
        key = ("d", st.dsem.num)
        in_.t.r[key] = st.dcnt
        if out.t.multi:
            out.t.w[key] = st.dcnt
        else:
            out.t.w = {key: st.dcnt}
            out.t.r = {}

    def barrier(self):
        for E in self.E.values():
            for P in self.E.values():
                if P.name != "sp" and P.insts:
                    self.need(E, ("e", P.name), len(P.insts) - 1)
            for st in self.dtiles:
                self.need(E, ("d", st.dsem.num), st.dcnt)

    def finish(self):
        E = self.E["sp"]
        for st in self.dtiles:
            self.need(E, ("d", st.dsem.num), st.dcnt)

    @staticmethod
    def _tiles(*vs):
        return [v.t for v in vs if isinstance(v, V)]

    @staticmethod
    def _a(v):
        return v.ap if isinstance(v, V) else v

    def mm(self, out, lhsT, rhs, start=True, stop=True):
        return self.op("pe", [lhsT.t, rhs.t], [out.t],
                       lambda e: e.matmul(out.ap, lhsT=lhsT.ap, rhs=rhs.ap, start=start, stop=stop))

    def tr(self, out, in_, ident):
        return self.op("pe", [in_.t, ident.t], [out.t],
                       lambda e: e.transpose(out.ap, in_.ap, ident.ap))

    def act(self, out, in_, func, bias=0.0, scale=1.0, extra=(), extra_w=()):
        return self.op("act", self._tiles(in_, bias, scale) + list(extra), [out.t] + list(extra_w),
                       lambda e: e.activation(out=out.ap, in_=in_.ap, func=func,
                                              bias=self._a(bias), scale=self._a(scale)))

    def tt(self, eng, out, in0, in1, op):
        return self.op(eng, [in0.t, in1.t], [out.t],
                       lambda e: e.tensor_tensor(out=out.ap, in0=in0.ap, in1=in1.ap, op=op))

    def ts(self, eng, out, in0, s1, op0, s2=None, op1=None):
        def f(e):
            if op1 is None:
                return e.tensor_scalar(out=out.ap, in0=in0.ap, scalar1=self._a(s1), scalar2=None, op0=op0)
            return e.tensor_scalar(out=out.ap, in0=in0.ap, scalar1=self._a(s1), scalar2=self._a(s2),
                                   op0=op0, op1=op1)
        return self.op(eng, self._tiles(in0, s1, s2), [out.t], f)

    def stt(self, eng, out, in0, scalar, in1, op0, op1):
        return self.op(eng, self._tiles(in0, scalar, in1), [out.t],
                       lambda e: e.scalar_tensor_tensor(out=out.ap, in0=in0.ap, scalar=self._a(scalar),
                                                        in1=in1.ap, op0=op0, op1=op1))

    def red(self, out, in_, op=None):
        return self.op("dve", [in_.t], [out.t],
                       lambda e: e.tensor_reduce(out=out.ap, in_=in_.ap, axis=AX.X, op=op or ALU.add))

    def recip(self, out, in_):
        return self.op("dve", [in_.t], [out.t], lambda e: e.reciprocal(out=out.ap, in_=in_.ap))

    def copy(self, eng, out, in_):
        if eng == "act":
            return self.act(out, in_, AF.Copy)
        return self.op(eng, [in_.t], [out.t], lambda e: e.tensor_copy(out.ap, in_.ap))

    def memset(self, eng, out, val):
        return self.op(eng, [], [out.t], lambda e: e.memset(out.ap, val))


def build(dbg=False, nlayers=NL, phases="atc", nta=NT):
    nc = bass.Bass("TRN2", target_bir_lowering=False)
    k = K(nc)
    EI = "ExternalInput"
    x_in = k.dram("x", [S, D], F32, EI)
    w_in_d = k.dram("w_in", [NL, D, DIN], F32, EI)
    w_out_d = k.dram("w_out", [NL, D, D], F32, EI)
    w_gu_d = k.dram("w_gu", [NL, D, 2 * DFF], F32, EI)
    w_dn_d = k.dram("w_dn", [NL, DFF, D], F32, EI)
    g1_d = k.dram("g1", [NL, D], F32, EI)
    g2_d = k.dram("g2", [NL, D], F32, EI)
    og_d = k.dram("og", [NL, D], F32, EI)
    qkg_d = k.dram("qkg", [NL, 1792], F32, EI)
    cw_d = k.dram("cw", [NL, 128, NCH, 3], F32, EI)
    cb_d = k.dram("cb", [NL, 128, NCH], F32, EI)
    ta_d = k.dram("ta", [NL, 4, 15, 64, 64], F32, EI)
    tabB_d = k.dram("tabB", [128, NT, 2, 8], F32, EI)
    tabC_d = k.dram("tabC", [128, NT, 2, 2, 16], F32, EI)
    tm_d = k.dram("tm", [128, 2944], F32, EI)
    y_out = k.dram("y", [S, D], F32, "ExternalOutput")
    sk = "ExternalOutput" if dbg else "Internal"
    qkt_d = k.dram("qkt", [15, 128, S], BF16, sk)
    vs_d = [k.dram("vsA", [128, NT, 4, 65], BF16, sk), k.dram("vsB", [128, NT, 6, 65], BF16, sk),
            k.dram("vsC", [128, NT, 2, 65], BF16, sk)]
    mixT_d = k.dram("mixT", [8, 128, S], BF16, sk)
    xr_d = k.dram("xr", [S, D], F32, sk)
    win_bf = k.dram("win_bf", [NL, D, DIN], BF16)
    wout_bf = k.dram("wout_bf", [NL, D, D], BF16)
    wdn_bf = k.dram("wdn_bf", [NL, DFF, D], BF16)

    with ExitStack() as g:
        PH = [g.enter_context(nc.psum_tensor(f"psd{i}", [128, 1024], F32)) for i in range(4)]
        PS = [Tile(PH[i // 2][:, (i % 2) * 512:(i % 2 + 1) * 512], f"ps{i}") for i in range(8)]
        for i in range(8):
            PS[i].pair = PH[i // 2]
        idb = k.sb(g, "idb", [128, 128], BF16)
        idf = k.sb(g, "idf", [128, 128], F32)
        k.memset("dve", idf[:], 0.0)
        k.op("pool", [idf], [idf], lambda e: e.affine_select(
            out=idf.base[:], in_=idf.base[:], pattern=[[-1, 128]], compare_op=ALU.not_equal,
            fill=1.0, base=0, channel_multiplier=1))
        k.copy("dve", idb[:], idf[:])
        C = dict(nlayers=nlayers, win_bf=win_bf, wout_bf=wout_bf, wdn_bf=wdn_bf, PS=PS, idb=idb, idf=idf, x_in=x_in, w_in_d=w_in_d, w_out_d=w_out_d, w_gu_d=w_gu_d,
                 w_dn_d=w_dn_d, g1_d=g1_d, g2_d=g2_d, og_d=og_d, qkg_d=qkg_d, cw_d=cw_d, cb_d=cb_d,
                 ta_d=ta_d, tabB_d=tabB_d, tabC_d=tabC_d, tm_d=tm_d, qkt_d=qkt_d, vs_d=vs_d,
                 mixT_d=mixT_d)
        for l in range(nlayers):
            x_src = x_in if l == 0 else xr_d
            x_dst = y_out if l == nlayers - 1 else xr_d
            if "a" in phases:
                phase_a(k, l, C, x_src, nta)
            if "t" in phases:
                phase_attn(k, l, C, mixers=("A", "B"))
            with ExitStack() as ws:
                k.open_scope()
                w_gu = None
                if "c" in phases:
                    w_gu = k.sb(ws, "w_gu_sb", [128, 8, 2 * DFF], BF16, multi=True)
                    for kc in range(8):
                        k.dma(w_gu[:, kc, :], w_gu_d[l, kc * 128:(kc + 1) * 128, :], q="pool")
                if "t" in phases:
                    phase_attn(k, l, C, mixers=("C",))
                if "c" in phases:
                    phase_cd(k, l, C, x_src, x_dst, w_gu)
                k.close_scope()
        k.finish()
    nc._kstats = {n: (len(e.insts), len(e.stamped)) for n, e in k.E.items()}
    return nc


def rmsnorm_tile(k, xin, g_bc, junk, ss, rs, hb):
    k.act(junk[:], xin, AF.Square)
    k.red(ss[:], junk[:])
    k.act(ss[:], ss[:], AF.Sqrt, bias=EPS, scale=1.0 / D)
    k.recip(rs[:], ss[:])
    k.stt("dve", hb[:], xin, rs[:, 0:1], g_bc[:], ALU.mult, ALU.mult)


def transpose8(k, PSb, idb, hb, dst_fn, eng="act"):
    pv = PSb[:].bitcast(BF16)
    for c in range(8):
        k.tr(pv[:, c * 128:(c + 1) * 128], hb[:, c * 128:(c + 1) * 128], idb[:])
    k.copy(eng, dst_fn(), pv[:, 0:1024].rearrange("p (c t) -> p c t", c=8))


def phase_a(k, l, C, x_src, nta=NT):
    PS, idb, w_in_d, qkt_d, vs_d = C["PS"], C["idb"], C["w_in_d"], C["qkt_d"], C["vs_d"]
    k.open_scope()
    with ExitStack() as st:
        tabB = k.sb(st, "tabB_sb", [128, NT, 2, 8], F32)
        tabC = k.sb(st, "tabC_sb", [128, NT, 2, 2, 16], F32)
        g1 = k.sb(st, "g1", [128, D], F32)
        qkg = k.sb(st, "qkg", [128, 1792], F32)
        k.dma(tabB[:], C["tabB_d"][:])
        k.dma(tabC[:], C["tabC_d"][:])
        k.dma(g1[:], C["g1_d"][l].partition_broadcast(128))
        k.dma(qkg[:], C["qkg_d"][l].partition_broadcast(128))
        w_in = k.sb(st, "w_in_sb", [128, 8, DIN], BF16, multi=True)
        for kc in range(8):
            if l == 0:
                k.dma(w_in[:, kc, :], w_in_d[l, kc * 128:(kc + 1) * 128, :], q="pool")
            else:
                k.dma(w_in[:, kc, :], C["win_bf"][l, kc * 128:(kc + 1) * 128, :])
        if l == 0:
            for l_ in range(C["nlayers"]):
                for r0 in range(0, D, 256):
                    k.dma(C["wout_bf"][l_, r0:r0 + 256, :], C["w_out_d"][l_, r0:r0 + 256, :], q="pool")
                for r0 in range(0, DFF, 256):
                    k.dma(C["wdn_bf"][l_, r0:r0 + 256, :], C["w_dn_d"][l_, r0:r0 + 256, :], q="pool")
                if l_ > 0:
                    for r0 in range(0, D, 128):
                        k.dma(C["win_bf"][l_, r0:r0 + 128, :], C["w_in_d"][l_, r0:r0 + 128, :], q="pool")
        xt = [k.sb(st, f"a_x{i}", [128, D], F32) for i in range(2)]
        junk = k.sb(st, "a_junk", [128, D], F32)
        ss = k.sb(st, "a_ss", [128, 1], F32)
        rs = k.sb(st, "a_rs", [128, 1], F32)
        hb = k.sb(st, "a_hb", [128, D], BF16)
        pr = [k.sb(st, f"a_pr{i}", [128, DIN], F32) for i in range(3)]
        sqb = k.sb(st, "a_sqb", [128, 1792], F32)
        ss28 = k.sb(st, "a_ss28", [128, 28], F32)
        rs28 = k.sb(st, "a_rs28", [128, 28], F32)
        rt = [k.sb(st, f"a_rt{i}", [128, 256], F32) for i in range(4)]
        qkb = [k.sb(st, f"a_qkb{i}", [128, 1920], BF16) for i in range(2)]
        vaug = [k.sb(st, f"a_vaug{i}", [128, 12, 65], BF16) for i in range(2)]
        qst = [k.sb(st, f"a_qst{i}", [128, 15, 512], BF16) for i in range(2)]
        for i in range(2):
            k.memset("pool", vaug[i][:, :, 64:65], 1.0)

        hT = [k.sb(st, f"a_hT3_{i}", [128, 8, 128], BF16) for i in range(3)]
        xt = xt + [k.sb(st, "a_x2", [128, D], F32)]

        def S1(t):
            x = xt[t % 3]
            k.dma(x[:], x_src[t * 128:(t + 1) * 128, :])
            rmsnorm_tile(k, x[:], g1, junk, ss, rs, hb)

        def S1t(t, c0, c1):
            pv = PS[5][:].bitcast(BF16)
            for c in range(c0, c1):
                k.tr(pv[:, c * 128:(c + 1) * 128], hb[:, c * 128:(c + 1) * 128], idb[:])
            if c1 == 8:
                k.copy("act", hT[t % 3][:], pv[:, 0:1024].rearrange("p (c t) -> p c t", c=8))

        def S2g(t, j):
            h = hT[t % 3]
            for kc in range(8):
                k.mm(PS[j][:, :], h[:, kc, :], w_in[:, kc, j * 512:(j + 1) * 512],
                     start=(kc == 0), stop=(kc == 7))

        def S2ev(t):
            p = pr[t % 3]
            for j in range(5):
                k.copy("act" if j % 2 == 0 else "dve", p[:, j * 512:(j + 1) * 512], PS[j][:, :])

        def S3(t):
            p = pr[t % 3]
            k.act(sqb[:], p[:, 0:1792], AF.Square)
            k.red(ss28[:], sqb[:].rearrange("p (h d) -> p h d", h=28))
            k.act(ss28[:], ss28[:], AF.Sqrt, bias=EPS, scale=1.0 / 64)
            k.recip(rs28[:], ss28[:])
            qv = p[:, 0:1792].rearrange("p (h d) -> p h d", h=28)
            k.tt("dve", qv, qv, rs28[:].unsqueeze(2).to_broadcast([128, 28, 64]), ALU.mult)

        def S3y(t):
            p = pr[t % 3]
            bv = p[:, 512:1280].rearrange("p (h d) -> p h d", h=12)
            gbv = qkg[:, 512:1280].rearrange("p (h d) -> p h d", h=12)
            k.tt("pool", bv[:, :, 0:16], bv[:, :, 0:16], gbv[:, :, 0:16], ALU.mult)
            k.tt("pool", p[:, 1280:1792], p[:, 1280:1792], qkg[:, 1280:1792], ALU.mult)
            x1, x2 = bv[:, :, 0:8], bv[:, :, 8:16]
            cB = tabB[:, t, 0, :].unsqueeze(1).to_broadcast([128, 12, 8])
            sB = tabB[:, t, 1, :].unsqueeze(1).to_broadcast([128, 12, 8])
            tv = [r[:, 0:96].rearrange("p (h d) -> p h d", h=12) for r in rt]
            k.tt("pool", tv[0], x1, cB, ALU.mult)
            k.tt("pool", tv[1], x2, sB, ALU.mult)
            k.tt("pool", tv[2], x2, cB, ALU.mult)
            k.tt("pool", tv[3], x1, sB, ALU.mult)
            k.tt("pool", x1, tv[0], tv[1], ALU.subtract)
            k.tt("pool", x2, tv[2], tv[3], ALU.add)
            cv = p[:, 1280:1792].rearrange("p (h a b d) -> p h a b d", h=8, a=2, b=2)
            y1, y2 = cv[:, :, :, 0, :], cv[:, :, :, 1, :]
            cC = tabC[:, t, 0, :, :].unsqueeze(1).to_broadcast([128, 8, 2, 16])
            sC = tabC[:, t, 1, :, :].unsqueeze(1).to_broadcast([128, 8, 2, 16])
            uv = [r[:, 0:256].rearrange("p (h a d) -> p h a d", h=8, a=2) for r in rt]
            k.tt("pool", uv[0], y1, cC, ALU.mult)
            k.tt("pool", uv[1], y2, sC, ALU.mult)
            k.tt("pool", uv[2], y2, cC, ALU.mult)
            k.tt("pool", uv[3], y1, sC, ALU.mult)
            k.tt("pool", y1, uv[0], uv[1], ALU.subtract)
            k.tt("pool", y2, uv[2], uv[3], ALU.add)
            qb = qkb[t % 2]
            k.tt("dve", qb[:, 0:512], p[:, 0:512], qkg[:, 0:512], ALU.mult)
            qbv = qb[:, 512:1280].rearrange("p (h d) -> p h d", h=12)
            k.tt("dve", qbv[:, :, 16:64], bv[:, :, 16:64], gbv[:, :, 16:64], ALU.mult)
            k.copy("pool", qbv[:, :, 0:16], bv[:, :, 0:16])
            k.copy("act", qb[:, 1280:1664], p[:, 1280:1664])
            k.copy("pool", qb[:, 1664:1920].rearrange("p (k u d) -> p k u d", k=2, u=2),
                   p[:, 1664:1792].rearrange("p (k u d) -> p k u d", k=2, u=1).to_broadcast([128, 2, 2, 64]))
            va = vaug[t % 2]
            k.copy("act", va[:, :, 0:64], p[:, 1792:2560].rearrange("p (h d) -> p h d", h=12))
            k.dma(vs_d[0][:, t, :, :], va[:, 0:4, :], q="act")
            k.dma(vs_d[1][:, t, :, :], va[:, 4:10, :], q="act")
            k.dma(vs_d[2][:, t, :, :], va[:, 10:12, :], q="act")

        def S3b(t, c0, c1):
            qb = qkb[t % 2]
            qs = qst[(t // 4) % 2]
            pv6 = PS[6][:].bitcast(BF16)
            pv7 = PS[7][:].bitcast(BF16)
            for c in range(c0, c1):
                dst = pv6[:, c * 128:(c + 1) * 128] if c < 8 else pv7[:, (c - 8) * 128:(c - 7) * 128]
                k.tr(dst, qb[:, c * 128:(c + 1) * 128], idb[:])
            if c1 < 15:
                return
            tl = (t % 4) * 128
            k.copy("dve", qs[:, 0:8, tl:tl + 128], pv6[:, 0:1024].rearrange("p (c t) -> p c t", c=8))
            k.copy("act", qs[:, 8:15, tl:tl + 128], pv7[:, 0:896].rearrange("p (c t) -> p c t", c=7))
            if t % 4 == 3:
                t0 = (t - 3) * 128
                k.dma(qkt_d[:, :, t0:t0 + 512].rearrange("c p t -> p c t"), qs[:], q="act")

        for i in range(-2, nta + 2):
            if 0 <= i < nta:
                S2ev(i)
            if 0 <= i + 2 < nta:
                S1(i + 2)
            if 0 <= i - 1 < nta:
                S3y(i - 1)
            if 0 <= i < nta:
                S3(i)
            a_ok, b_ok, c_ok = 0 <= i + 1 < nta, 0 <= i - 2 < nta, 0 <= i + 2 < nta
            if b_ok:
                S3b(i - 2, 0, 5)
            if a_ok:
                S2g(i + 1, 0)
            if b_ok:
                S3b(i - 2, 5, 10)
            if a_ok:
                S2g(i + 1, 1)
            if b_ok:
                S3b(i - 2, 10, 15)
            if a_ok:
                S2g(i + 1, 2)
            if c_ok:
                S1t(i + 2, 0, 4)
            if a_ok:
                S2g(i + 1, 3)
            if c_ok:
                S1t(i + 2, 4, 8)
            if a_ok:
                S2g(i + 1, 4)
        k.close_scope()


def phase_attn(k, l, C, mixers=("A", "B", "C")):
    PS, idb, idf, ta_d, qkt_d, vs_d, mixT_d = (C["PS"], C["idb"], C["idf"], C["ta_d"], C["qkt_d"],
                                                C["vs_d"], C["mixT_d"])
    for mixer in mixers:
        k.open_scope()
        with ExitStack() as st:
            og = k.sb(st, "og", [128, D], F32)
            k.dma(og[:], C["og_d"][l].partition_broadcast(128))
            if mixer == "B":
                tm = k.sb(st, "tm_sb", [128, 2944], BF16)
                k.dma(tm[:], C["tm_d"][:], q="pool")
            if mixer == "A":
                nh, qc0, vh0, nvh, mc0, oc0 = 4, 0, 0, 4, 0, 0
                chunks = [0, 1, 2, 3]
                kchunk = lambda h: 2 + h // 2
                vhead = lambda h: h
                segs = [(0, 256, list(range(0, 4)), "full"), (256, 256, list(range(0, 6)), "int")]
                for qb in range(1, 7):
                    segs.append((512 * qb, 512, list(range(4 * qb - 2, 4 * qb + 6)), "int"))
                segs.append((3584, 320, list(range(26, 32)), "int"))
                segs.append((3904, 192, list(range(28, 32)), "full"))
            elif mixer == "B":
                nh, qc0, vh0, nvh, mc0, oc0 = 6, 4, 4, 6, 2, 256
                chunks = [4, 5, 6, 7, 8, 9]
                kchunk = lambda h: 7 + h // 2
                vhead = lambda h: h
                segs = [(512 * qb, 512, list(range(max(0, 4 * qb - 8), min(32, 4 * qb + 12))), "band")
                        for qb in range(8)]
            else:
                nh, qc0, vh0, nvh, mc0, oc0 = 6, 10, 10, 2, 5, 640
                chunks = [10, 11, 12, 13, 14]
                kchunk = lambda h: 13 + h // 3
                vhead = lambda h: h // 3
                segs = [(512 * qb, 512, list(range(32)), None) for qb in range(8)]
            W = nh * 64
            nmc = W // 128
            qk = {}
            for c in chunks:
                if c >= qc0 + (nh + 1) // 2:
                    qk[c] = k.sb(st, f"at_qk{c}", [128, S], BF16)
                    k.dma(qk[c][:], qkt_d[c, :, :])
            qz = []
            for h in range(nh):
                t_ = k.sb(st, f"at_qz{h}", [128, S], BF16, multi=True)
                b_ = (h % 2) * 64
                k.memset("pool", t_[64 - b_:128 - b_, :], 0.0)
                k.dma(t_[b_:b_ + 64, :], qkt_d[qc0 + h // 2, b_:b_ + 64, :])
                qz.append(t_)
            vsb = k.sb(st, "at_v", [128, NT, nvh, 65], BF16)
            k.dma(vsb[:], vs_d["ABC".index(mixer)][:])
            oT = [k.sb(st, f"at_oT{i}", [65, 512], F32) for i in range(2)]
            rz = [k.sb(st, f"at_rz{i}", [128, 4], F32) for i in range(2)]
            att = [k.sb(st, f"at_att{i}", [128, 4, W], F32) for i in range(2)]
            junk = k.sb(st, "at_junk", [128, W], F32)
            ssn = k.sb(st, "at_ssn", [128, 4], F32)
            rsn = k.sb(st, "at_rsn", [128, 4], F32)
            mixb = k.sb(st, "at_mixb", [128, 4, W], BF16)
            mst = [k.sb(st, f"at_mst{i}", [128, nmc, 512], BF16) for i in range(2)]
            if mixer == "A":
                traw = k.sb(st, "at_traw", [128, 15, 64], F32, multi=True)
                etf = [k.sb(st, f"at_etf{h}", [128, 32, 64], BF16) for h in range(4)]
                eti = [k.sb(st, f"at_eti{h}", [128, 32, 64], BF16) for h in range(4)]
                for h in range(4):
                    src = ta_d[l, h].rearrange("i k c -> k i c")
                    k.dma(traw[0:64, :, :], src)
                    k.dma(traw[64:128, :, :], src)
                    k.memset("pool", etf[h][:], 0.0)
                    k.memset("pool", eti[h][:], 0.0)
                    k.act(etf[h][0:64, 8:23, :], traw[0:64, :, :], AF.Exp)
                    k.act(etf[h][64:128, 9:24, :], traw[64:128, :, :], AF.Exp)
                    k.copy("pool", eti[h][0:64, 12:20, :], etf[h][0:64, 12:20, :])
                    k.copy("pool", eti[h][64:128, 13:21, :], etf[h][64:128, 13:21, :])
            pT2 = []
            for i in range(3):
                hnd = st.enter_context(k.nc.sbuf_tensor(f"at_pTp{i}_{l}_{mixer}", [128, 2, 512], BF16))
                pT2.append((hnd, Tile(hnd[:, 0, :], f"pTa{i}"), Tile(hnd[:, 1, :], f"pTb{i}")))
            SP = [(PS[0], PS[1]), (PS[2], PS[3])]
            PO = [PS[4], PS[5]]
            items = []
            for si, (q0, nq, kts, mk) in enumerate(segs):
                for h in range(nh):
                    npair = len(kts) // 2
                    assert len(kts) % 2 == 0
                    for pi in range(npair):
                        items.append((si, h, pi, npair, kts[2 * pi], kts[2 * pi + 1]))
            deferred = []

            def issue_S(ii):
                si, h, pi, npair, ka, kb = items[ii]
                q0, nq, kts, mk = segs[si]
                kT = qk[kchunk(h)]
                for s_, kt in enumerate((ka, kb)):
                    k.mm(SP[ii % 2][s_][:, 0:nq], kT[:, kt * 128:(kt + 1) * 128], qz[h][:, q0:q0 + nq])

            junk4 = k.sb(st, "at_junk4", [128, 4, W], F32)

            def sched(due, fn):
                pos = len(deferred)
                while pos > 0 and deferred[pos - 1][0] > due:
                    pos -= 1
                deferred.insert(pos, (due, fn))

            def epi_head(ii, si, h, sc):
                q0, nq, kts, mk = segs[si]
                nsub = (nq + 127) // 128
                at = att[si % 2]
                ot = oT[sc % 2]
                rzz = rz[sc % 2]
                pt = PS[6]
                ptv = pt[:, 0:nsub * 65].rearrange("p (i e) -> p i e", e=65)

                def s1():
                    for i in range(nsub):
                        n = min(128, nq - i * 128)
                        k.tr(pt[0:n, i * 65:(i + 1) * 65], ot[0:65, i * 128:i * 128 + n], idf[0:65, 0:65])

                def s2():
                    k.recip(rzz[:, 0:nsub], ptv[:, :, 64])

                def s3():
                    k.tt("dve", at[:, 0:nsub, h * 64:(h + 1) * 64], ptv[:, :, 0:64],
                         rzz[:, 0:nsub].unsqueeze(2).to_broadcast([128, nsub, 64]), ALU.mult)
                sched(ii + 1, s1)
                sched(ii + 2, s2)
                sched(ii + 3, s3)

            def epi_seg(ii, si):
                q0, nq, kts, mk = segs[si]
                nsub = (nq + 127) // 128
                at = att[si % 2]
                ms = mst[si % 2]
                pm = PS[7][:].bitcast(BF16)
                atv = at[:, 0:nsub, :]

                def t0():
                    k.act(junk4[:, 0:nsub, :], atv, AF.Square)

                def t1_():
                    k.red(ssn[:, 0:nsub], junk4[:, 0:nsub, :])

                def t2():
                    k.act(ssn[:, 0:nsub], ssn[:, 0:nsub], AF.Sqrt, bias=EPS, scale=1.0 / W)

                def t3():
                    k.recip(rsn[:, 0:nsub], ssn[:, 0:nsub])

                def t4():
                    k.tt("dve", atv, atv, rsn[:, 0:nsub].unsqueeze(2).to_broadcast([128, nsub, W]), ALU.mult)

                def t5():
                    k.tt("dve", mixb[:, 0:nsub, :], atv,
                         og[:, oc0:oc0 + W].unsqueeze(1).to_broadcast([128, nsub, W]), ALU.mult)

                def trs(i0, i1):
                    def f():
                        for i in range(i0, i1):
                            n = min(128, nq - i * 128)
                            for c in range(nmc):
                                o_ = ((i - i0) * nmc + c) * 128
                                k.tr(pm[:, o_:o_ + n], mixb[0:n, i, c * 128:(c + 1) * 128], idb[0:n, 0:n])
                    return f

                def cps(i0, i1):
                    def f():
                        for i in range(i0, i1):
                            n = min(128, nq - i * 128)
                            o_ = (i - i0) * nmc * 128
                            k.copy("dve", ms[:, :, i * 128:i * 128 + n],
                                   pm[:, o_:o_ + nmc * 128].rearrange("p (c t) -> p c t", c=nmc)[:, :, 0:n])
                    return f

                def st_():
                    k.dma(mixT_d[mc0:mc0 + nmc, :, q0:q0 + nq].rearrange("c p t -> p c t"), ms[:, :, 0:nq],
                          q="pool")
                steps = [t0, t1_, t2, t3, t4, t5, trs(0, min(2, nsub)), cps(0, min(2, nsub))]
                if nsub > 2:
                    steps += [trs(2, nsub), cps(2, nsub)]
                steps.append(st_)
                for j, f in enumerate(steps):
                    sched(ii + 4 + j, f)

            issue_S(0)
            scnt = 0
            mcnt = 0
            for ii, (si, h, pi, npair, ka, kb) in enumerate(items):
                q0, nq, kts, mk = segs[si]
                if ii + 1 < len(items):
                    issue_S(ii + 1)
                phnd, pta, ptb = pT2[ii % 3]
                sa, sb_ = SP[ii % 2]
                spv = V(sa.pair[:, :].rearrange("p (b n) -> p b n", b=2)[:, :, 0:nq], sa)
                k.act(V(phnd[:, :, 0:nq], pta), spv, AF.Exp, scale=0.125, extra=[sb_], extra_w=[ptb])
                po = PO[scnt % 2]
                for s_, kt in enumerate((ka, kb)):
                    p = (pta, ptb)[s_]
                    meng = "dve" if (mcnt % 2 == 0 or mk == "band") else "pool"
                    mcnt += 1
                    if mk == "band":
                        u0 = 1408 - (kt * 128 - q0)
                        k.tt(meng, p[:, 0:nq], p[:, 0:nq], tm[:, u0:u0 + nq], ALU.mult)
                    elif mk in ("full", "int"):
                        tab = etf[h] if mk == "full" else eti[h]
                        rq0 = q0 // 64
                        nr = nq // 64
                        s0 = 7 - 2 * kt + rq0 + 8
                        assert 0 <= s0 and s0 + nr <= 32, (s0, nr, kt, q0)
                        pv = p[:, 0:nq].rearrange("p (j c) -> p j c", c=64)
                        k.tt(meng, pv, pv, tab[:, s0:s0 + nr, :], ALU.mult)
                    k.mm(po[0:65, 0:nq], vsb[:, kt, vhead(h), :], p[:, 0:nq],
                         start=(pi == 0 and s_ == 0), stop=(pi == npair - 1 and s_ == 1))
                if pi == npair - 1:
                    k.copy("dve", oT[scnt % 2][:, 0:nq], po[0:65, 0:nq])
                    epi_head(ii, si, h, scnt)
                    if h == nh - 1:
                        epi_seg(ii, si)
                    scnt += 1
                while deferred and deferred[0][0] <= ii:
                    deferred.pop(0)[1]()
            while deferred:
                deferred.pop(0)[1]()
            k.close_scope()


def phase_cd(k, l, C, x_src, x_dst, w_gu):
    PS, idb, w_out_d, w_gu_d, w_dn_d, mixT_d = (C["PS"], C["idb"], C["w_out_d"], C["w_gu_d"], C["w_dn_d"],
                                                 C["mixT_d"])
    k.open_scope()
    with ExitStack() as st:
        g2 = k.sb(st, "g2", [128, D], F32)
        cw = k.sb(st, "cw", [128, NCH, 3], F32)
        cb = k.sb(st, "cb", [128, NCH], F32)
        k.dma(g2[:], C["g2_d"][l].partition_broadcast(128))
        k.dma(cw[:], C["cw_d"][l])
        k.dma(cb[:], C["cb_d"][l])
        w_out = k.sb(st, "w_out_sb", [128, 8, D], BF16, multi=True)
        w_dn = k.sb(st, "w_dn_sb", [128, NCH, D], BF16, multi=True)
        for kc in range(8):
            k.dma(w_out[:, kc, :], C["wout_bf"][l, kc * 128:(kc + 1) * 128, :])
        for cc in range(NCH):
            k.dma(w_dn[:, cc, :], C["wdn_bf"][l, cc * 128:(cc + 1) * 128, :])
        mT = [k.sb(st, f"c_mT{i}", [128, 8, 128], BF16) for i in range(2)]
        x1 = [k.sb(st, f"c_x1_{i}", [128, D], F32) for i in range(6)]
        junk = k.sb(st, "c_junk", [128, D], F32)
        ss = k.sb(st, "c_ss", [128, 1], F32)
        rs = k.sb(st, "c_rs", [128, 1], F32)
        hb = k.sb(st, "c_hb", [128, D], BF16)
        h2T = [k.sb(st, f"c_h2T{i}", [128, 8, TB + 2], BF16) for i in range(3)]
        t1 = [k.sb(st, f"c_t1_{i}", [128, TB], F32) for i in range(2)]
        ge = [k.sb(st, f"c_ge{i}", [128, TB], F32) for i in range(2)]
        mm_ = [k.sb(st, f"c_m{i}", [128, TB], BF16) for i in range(2)]

        def stage1a(b, tt, nb, ps):
            t = b * 2 + tt
            xx = x1[(b % 3) * 2 + tt]
            m = mT[t % 2]
            if nb == 0:
                k.dma(xx[:], x_src[t * 128:(t + 1) * 128, :])
                k.dma(m[:], mixT_d[:, :, t * 128:(t + 1) * 128].rearrange("c p t -> p c t"))
            for kc in range(8):
                k.mm(ps[:, :], m[:, kc, :], w_out[:, kc, nb * 512:(nb + 1) * 512],
                     start=(kc == 0), stop=(kc == 7))
            k.tt("dve", xx[:, nb * 512:(nb + 1) * 512], ps[:, :], xx[:, nb * 512:(nb + 1) * 512], ALU.add)
            if nb == 1:
                rmsnorm_tile(k, xx[:], g2, junk, ss, rs, hb)

        def stage1b(b, tt, pst, halo):
            hT = h2T[b % 3]
            transpose8(k, pst, idb, hb, lambda: hT[:, :, 1 + tt * 128:1 + (tt + 1) * 128], eng="act")
            if not halo:
                return
            if b == 0:
                k.memset("pool", hT[:, :, 0:1], 0.0)
            else:
                hp = h2T[(b - 1) % 3]
                k.copy("pool", hT[:, :, 0:1], hp[:, :, TB:TB + 1])
                k.copy("pool", hp[:, :, TB + 1:TB + 2], hT[:, :, 1:2])
            if b == NB - 1:
                k.memset("pool", hT[:, :, TB + 1:TB + 2], 0.0)

        def stage1(b):
            for tt in range(2):
                stage1a(b, tt, 0, PS[4])
                stage1a(b, tt, 1, PS[5])
                stage1b(b, tt, PS[6], tt == 1)

        PG = [PS[0], PS[1]]
        PU = [PS[2], PS[3]]

        def stage2(b, mid=None):
            hT = h2T[b % 3]

            def g_mm(cc):
                pg = PG[cc % 2]
                for kc in range(8):
                    k.mm(pg[:, 0:TB + 2], w_gu[:, kc, cc * 128:(cc + 1) * 128], hT[:, kc, 0:TB + 2],
                         start=(kc == 0), stop=(kc == 7))

            def u_mm(cc):
                pu = PU[cc % 2]
                for kc in range(8):
                    k.mm(pu[:, 0:TB], w_gu[:, kc, DFF + cc * 128:DFF + (cc + 1) * 128], hT[:, kc, 1:TB + 1],
                         start=(kc == 0), stop=(kc == 7))

            def conv(cc):
                pg = PG[cc % 2]
                a = t1[cc % 2]
                k.act(a[:], pg[:, 1:TB + 1], AF.Identity, bias=cb[:, cc:cc + 1], scale=cw[:, cc, 1:2])
                k.stt("dve", a[:], pg[:, 0:TB], cw[:, cc, 0:1], a[:], ALU.mult, ALU.add)
                k.stt("dve", a[:], pg[:, 2:TB + 2], cw[:, cc, 2:3], a[:], ALU.mult, ALU.add)
                k.act(ge[cc % 2][:], a[:], AF.Gelu)

            def mult(cc):
                k.tt("dve", mm_[cc % 2][:], ge[cc % 2][:], PU[cc % 2][:, 0:TB], ALU.mult)

            def down(cc):
                m = mm_[cc % 2]
                for tt in range(2):
                    for nb in range(2):
                        k.mm(PS[4 + tt * 2 + nb][:, :], m[:, tt * 128:(tt + 1) * 128],
                             w_dn[:, cc, nb * 512:(nb + 1) * 512], start=(cc == 0), stop=(cc == NCH - 1))

            g_mm(0)
            g_mm(1)
            u_mm(0)
            conv(0)
            mult(0)
            for cc in range(NCH):
                if cc + 2 < NCH:
                    g_mm(cc + 2)
                if cc + 1 < NCH:
                    u_mm(cc + 1)
                    conv(cc + 1)
                    mult(cc + 1)
                down(cc)
                if mid is not None and cc in mid:
                    mid[cc](PU[cc % 2])
            for tt in range(2):
                t = b * 2 + tt
                xx = x1[(b % 3) * 2 + tt]
                for nb in range(2):
                    k.tt("dve", xx[:, nb * 512:(nb + 1) * 512], PS[4 + tt * 2 + nb][:, :],
                         xx[:, nb * 512:(nb + 1) * 512], ALU.add)
                k.dma(x_dst[t * 128:(t + 1) * 128, :], xx[:], q="pool")

        stage1(0)
        stage1(1)
        for b in range(NB):
            mid = None
            if b + 2 < NB:
                mid = {2: (lambda ps, b=b: stage1a(b + 2, 0, 0, ps)),
                       4: (lambda ps, b=b: stage1a(b + 2, 0, 1, ps)),
                       7: (lambda ps, b=b: stage1b(b + 2, 0, ps, False)),
                       10: (lambda ps, b=b: stage1a(b + 2, 1, 0, ps)),
                       12: (lambda ps, b=b: stage1a(b + 2, 1, 1, ps)),
                       15: (lambda ps, b=b: stage1b(b + 2, 1, ps, True))}
            stage2(b, mid)
    k.close_scope()


def _rope(pos, dim, theta):
    inv = (np.float32(theta) ** (-np.arange(0, dim, 2, dtype=np.float32) / np.float32(dim))).astype(np.float32)
    ang = pos.astype(np.float32)[:, None] * inv[None, :]
    return np.cos(ang).astype(np.float32), np.sin(ang).astype(np.float32)


def _consts():
    t = np.arange(S)
    c1, s1 = _rope(t, 16, 500000.0)
    cr, sr = _rope(t // 64, 32, 10000.0)
    cc, sc = _rope(t % 64, 32, 10000.0)
    tabB = np.stack([c1, s1], axis=1)
    tabB = tabB.reshape(NT, 128, 2, 8).transpose(1, 0, 2, 3)
    tabC = np.stack([np.stack([cr, cc], axis=1), np.stack([sr, sc], axis=1)], axis=1)
    tabC = tabC.reshape(NT, 128, 2, 2, 16).transpose(1, 0, 2, 3, 4)
    kl = np.arange(128)[:, None]
    u = np.arange(2944)[None, :]
    dlt = kl - u + 1408
    a = np.abs(dlt)
    m = (a <= 64).astype(np.float32) + ((a <= 256) & (dlt % 4 == 0)) + ((a <= 1024) & (dlt % 16 == 0))
    return (np.ascontiguousarray(tabB, dtype=np.float32), np.ascontiguousarray(tabC, dtype=np.float32),
            np.ascontiguousarray(m, dtype=np.float32))


def _prep(inp):
    f = lambda a: np.ascontiguousarray(np.asarray(a), dtype=np.float32)
    perm = np.concatenate([np.arange(0, 256), np.arange(256, 512), np.arange(768, 1152), np.arange(1152, 1536),
                           np.arange(1920, 2304), np.arange(2304, 2432), np.arange(512, 768),
                           np.arange(1536, 1920), np.arange(2432, 2560)])
    qg, kg = f(inp["q_norm_g"]), f(inp["k_norm_g"])
    qkg = np.stack([np.concatenate([np.tile(qg[l, 0], 4), np.tile(kg[l, 0], 4), np.tile(qg[l, 1], 6),
                                    np.tile(kg[l, 1], 6), np.tile(qg[l, 2], 6), np.tile(kg[l, 2], 2)])
                    for l in range(NL)])
    cwv = f(inp["conv_w"])
    cw = cwv.transpose(0, 2, 1).reshape(NL, NCH, 128, 3).transpose(0, 2, 1, 3)
    cb = f(inp["conv_b"]).reshape(NL, NCH, 128).transpose(0, 2, 1)
    rpb = f(inp["rpb"])
    rp = np.concatenate([rpb.reshape(NL, 4, 15 * 31), np.full((NL, 4, 1), NEG, np.float32)], axis=2)
    idx = np.arange(15)[:, None, None]
    kc = np.arange(64)[None, :, None]
    c = np.arange(64)[None, None, :]
    c0 = np.clip(c - 8, 0, 48)
    valid = (kc >= c0) & (kc < c0 + 16)
    flat = (14 - idx) * 31 + np.clip(kc - c + 15, 0, 30)
    flat = np.where(valid, flat, 15 * 31)
    ta = rp[:, :, flat]
    tabB, tabC, tm = _consts()
    shared = {
        "w_in": np.ascontiguousarray(f(inp["w_in"])[:, :, perm]),
        "w_out": f(inp["w_out"]), "w_gu": f(inp["w_gate_up"]), "w_dn": f(inp["w_down"]),
        "g1": f(inp["norm1_g"]), "g2": f(inp["norm2_g"]), "og": f(inp["out_norm_g"]),
        "qkg": np.ascontiguousarray(qkg, dtype=np.float32),
        "cw": np.ascontiguousarray(cw), "cb": np.ascontiguousarray(cb),
        "ta": np.ascontiguousarray(ta, dtype=np.float32),
        "tabB": tabB, "tabC": tabC, "tm": tm,
    }
    return shared


def kernel(**inputs):
    x = np.asarray(inputs["x"], dtype=np.float32)
    shared = _prep(inputs)
    nc = build()
    in_maps = [dict(shared, x=np.ascontiguousarray(x[b])) for b in range(8)]
    res = run_bass_kernel_spmd(nc, in_maps, core_ids=list(range(8)))
    return np.stack([np.asarray(r["y"], dtype=np.float32) for r in res.results], axis=0)
```

```python
import bisect
from contextlib import ExitStack
import numpy as np
import concourse.bass as bass
import concourse.mybir as mybir
from concourse.bass_utils import run_bass_kernel_spmd

F32 = mybir.dt.float32
BF16 = mybir.dt.bfloat16
AF = mybir.ActivationFunctionType
ALU = mybir.AluOpType
AX = mybir.AxisListType

S = 4096
D = 1024
NL = 2
DIN = 2560
DFF = 2816
NCH = DFF // 128
NT = S // 128
EPS = 1e-6
NEG = -30000.0
TB = 256
NB = S // TB


class V:
    def __init__(self, ap, t):
        self.ap = ap
        self.t = t

    def __getitem__(self, idx):
        return V(self.ap[idx], self.t)

    def rearrange(self, s, **kw):
        return V(self.ap.rearrange(s, **kw), self.t)

    def unsqueeze(self, a):
        return V(self.ap.unsqueeze(a), self.t)

    def to_broadcast(self, shp):
        return V(self.ap.to_broadcast(shp), self.t)

    def bitcast(self, dt):
        return V(self.ap.bitcast(dt), self.t)

    def partition_broadcast(self, n):
        return V(self.ap.partition_broadcast(n), self.t)


class Tile:
    def __init__(self, base, name, multi=False):
        self.base = base
        self.name = name
        self.w = {}
        self.r = {}
        self.dsem = None
        self.dcnt = 0
        self.multi = multi

    def __getitem__(self, idx):
        return V(self.base[idx], self)


class Eng:
    def __init__(self, name, eng, sem):
        self.name = name
        self.eng = eng
        self.sem = sem
        self.insts = []
        self.stamped = []
        self.seen = {}


class K:
    def __init__(self, nc):
        self.nc = nc
        self.E = {}
        for name, eng in (("pe", nc.tensor), ("act", nc.scalar), ("dve", nc.vector),
                          ("pool", nc.gpsimd), ("sp", nc.sync)):
            self.E[name] = Eng(name, eng, nc.alloc_semaphore(name="sem_" + name))
        self.dsems = {}
        self.dtiles = []
        self.free_slots = []
        self.scopes = []

    def sb(self, stack, name, shape, dt, multi=False):
        self.uid = getattr(self, "uid", 0) + 1
        name = "%s_u%d" % (name, self.uid)
        h = stack.enter_context(self.nc.sbuf_tensor(name, shape, dt))
        t = Tile(h, name, multi)
        if self.scopes:
            self.scopes[-1].append(t)
        return t

    def open_scope(self):
        self.scopes.append([])

    def close_scope(self):
        self.barrier()
        for t in self.scopes.pop():
            if t.dsem is not None:
                self.free_slots.append((t.dsem, t.dcnt))
                self.dtiles.remove(t)
                t.dsem = None

    def dram(self, name, shape, dt, kind="Internal"):
        h = self.nc.dram_tensor(name, shape, dt, kind=kind)
        return Tile(h.ap(), name, multi=True)

    def need(self, E, key, v):
        if key[0] == "e":
            P = self.E[key[1]]
            pos = bisect.bisect_left(P.stamped, v)
            if pos < len(P.stamped):
                val = pos + 1
            else:
                P.insts[v].then_inc(P.sem, 1)
                P.stamped.append(v)
                val = len(P.stamped)
            sem = P.sem
        else:
            sem = self.dsems[key[1]]
            val = v
        if E.seen.get(key, 0) >= val:
            return
        E.eng.wait_ge(sem, val)
        E.seen[key] = val

    def op(self, ename, reads, writes, fn):
        E = self.E[ename]
        me = ("e", ename)
        for t in reads:
            for key, v in list(t.w.items()):
                if key == me and ename == "pe":
                    continue
                self.need(E, key, v)
        for t in writes:
            for key, v in list(t.w.items()) + list(t.r.items()):
                if key == me:
                    continue
                self.need(E, key, v)
        inst = fn(E.eng)
        idx = len(E.insts)
        E.insts.append(inst)
        for t in reads:
            t.r[me] = idx
        for t in writes:
            if t.multi:
                t.w[me] = idx
            else:
                t.w = {me: idx}
                t.r = {}
        return inst

    def dma(self, out, in_, q="sp"):
        E = self.E[q]
        for key, v in list(in_.t.w.items()):
            self.need(E, key, v)
        deps = list(out.t.r.items())
        if not out.t.multi:
            deps += list(out.t.w.items())
        for key, v in deps:
            self.need(E, key, v)
        st = out.t if not isinstance(out.t.base, bass.AP) else in_.t
        if st.dsem is None:
            if self.free_slots:
                st.dsem, st.dcnt = self.free_slots.pop()
            else:
                st.dsem = self.nc.alloc_semaphore(name="d%d" % len(self.dsems))
                self.dsems[st.dsem.num] = st.dsem
            self.dtiles.append(st)
        st.dcnt += 16
        E.eng.dma_start(out=out.ap, in_=in_.ap).then_inc(st.dsem, 16)
        key = ("d", st.dsem.num)
        in_.t.r[key] = st.dcnt
        if out.t.multi:
            out.t.w[key] = st.dcnt
        else:
            out.t.w = {key: st.dcnt}
            out.t.r = {}

    def barrier(self):
        for E in self.E.values():
            for P in self.E.values():
                if P.name != "sp" and P.insts:
                    self.need(E, ("e", P.name), len(P.insts) - 1)
            for st in self.dtiles:
                self.need(E, ("d", st.dsem.num), st.dcnt)

    def finish(self):
        E = self.E["sp"]
        for st in self.dtiles:
            self.need(E, ("d", st.dsem.num), st.dcnt)

    @staticmethod
    def _tiles(*vs):
        return [v.t for v in vs if isinstance(v, V)]

    @staticmethod
    def _a(v):
        return v.ap if isinstance(v, V) else v

    def mm(self, out, lhsT, rhs, start=True, stop=True):
        return self.op("pe", [lhsT.t, rhs.t], [out.t],
                       lambda e: e.matmul(out.ap, lhsT=lhsT.ap, rhs=rhs.ap, start=start, stop=stop))

    def tr(self, out, in_, ident):
        return self.op("pe", [in_.t, ident.t], [out.t],
                       lambda e: e.transpose(out.ap, in_.ap, ident.ap))

    def act(self, out, in_, func, bias=0.0, scale=1.0, extra=(), extra_w=()):
        return self.op("act", self._tiles(in_, bias, scale) + list(extra), [out.t] + list(extra_w),
                       lambda e: e.activation(out=out.ap, in_=in_.ap, func=func,
                                              bias=self._a(bias), scale=self._a(scale)))

    def tt(self, eng, out, in0, in1, op):
        return self.op(eng, [in0.t, in1.t], [out.t],
                       lambda e: e.tensor_tensor(out=out.ap, in0=in0.ap, in1=in1.ap, op=op))

    def ts(self, eng, out, in0, s1, op0, s2=None, op1=None):
        def f(e):
            if op1 is None:
                return e.tensor_scalar(out=out.ap, in0=in0.ap, scalar1=self._a(s1), scalar2=None, op0=op0)
            return e.tensor_scalar(out=out.ap, in0=in0.ap, scalar1=self._a(s1), scalar2=self._a(s2),
                                   op0=op0, op1=op1)
        return self.op(eng, self._tiles(in0, s1, s2), [out.t], f)

    def stt(self, eng, out, in0, scalar, in1, op0, op1):
        return self.op(eng, self._tiles(in0, scalar, in1), [out.t],
                       lambda e: e.scalar_tensor_tensor(out=out.ap, in0=in0.ap, scalar=self._a(scalar),
                                                        in1=in1.ap, op0=op0, op1=op1))

    def red(self, out, in_, op=None):
        return self.op("dve", [in_.t], [out.t],
                       lambda e: e.tensor_reduce(out=out.ap, in_=in_.ap, axis=AX.X, op=op or ALU.add))

    def recip(self, out, in_):
        return self.op("dve", [in_.t], [out.t], lambda e: e.reciprocal(out=out.ap, in_=in_.ap))

    def copy(self, eng, out, in_):
        if eng == "act":
            return self.act(out, in_, AF.Copy)
        return self.op(eng, [in_.t], [out.t], lambda e: e.tensor_copy(out.ap, in_.ap))

    def memset(self, eng, out, val):
        return self.op(eng, [], [out.t], lambda e: e.memset(out.ap, val))


def build(dbg=False, nlayers=NL, phases="atc", nta=NT):
    nc = bass.Bass("TRN2", target_bir_lowering=False)
    k = K(nc)
    EI = "ExternalInput"
    x_in = k.dram("x", [S, D], F32, EI)
    w_in_d = k.dram("w_in", [NL, D, DIN], F32, EI)
    w_out_d = k.dram("w_out", [NL, D, D], F32, EI)
    w_gu_d = k.dram("w_gu", [NL, D, 2 * DFF], F32, EI)
    w_dn_d = k.dram("w_dn", [NL, DFF, D], F32, EI)
    g1_d = k.dram("g1", [NL, D], F32, EI)
    g2_d = k.dram("g2", [NL, D], F32, EI)
    og_d = k.dram("og", [NL, D], F32, EI)
    qkg_d = k.dram("qkg", [NL, 1792], F32, EI)
    cw_d = k.dram("cw", [NL, 128, NCH, 3], F32, EI)
    cb_d = k.dram("cb", [NL, 128, NCH], F32, EI)
    ta_d = k.dram("ta", [NL, 4, 15, 64, 64], F32, EI)
    tabB_d = k.dram("tabB", [128, NT, 2, 8], F32, EI)
    tabC_d = k.dram("tabC", [128, NT, 2, 2, 16], F32, EI)
    tm_d = k.dram("tm", [128, 2944], F32, EI)
    y_out = k.dram("y", [S, D], F32, "ExternalOutput")
    sk = "ExternalOutput" if dbg else "Internal"
    qkt_d = k.dram("qkt", [15, 128, S], BF16, sk)
    vs_d = [k.dram("vsA", [128, NT, 4, 65], BF16, sk), k.dram("vsB", [128, NT, 6, 65], BF16, sk),
            k.dram("vsC", [128, NT, 2, 65], BF16, sk)]
    mixT_d = k.dram("mixT", [8, 128, S], BF16, sk)
    xr_d = k.dram("xr", [S, D], F32, sk)
    win_bf = k.dram("win_bf", [NL, D, DIN], BF16)
    wout_bf = k.dram("wout_bf", [NL, D, D], BF16)
    wdn_bf = k.dram("wdn_bf", [NL, DFF, D], BF16)

    with ExitStack() as g:
        PH = [g.enter_context(nc.psum_tensor(f"psd{i}", [128, 1024], F32)) for i in range(4)]
        PS = [Tile(PH[i // 2][:, (i % 2) * 512:(i % 2 + 1) * 512], f"ps{i}") for i in range(8)]
        for i in range(8):
            PS[i].pair = PH[i // 2]
        idb = k.sb(g, "idb", [128, 128], BF16)
        idf = k.sb(g, "idf", [128, 128], F32)
        k.memset("dve", idf[:], 0.0)
        k.op("pool", [idf], [idf], lambda e: e.affine_select(
            out=idf.base[:], in_=idf.base[:], pattern=[[-1, 128]], compare_op=ALU.not_equal,
            fill=1.0, base=0, channel_multiplier=1))
        k.copy("dve", idb[:], idf[:])
        C = dict(nlayers=nlayers, win_bf=win_bf, wout_bf=wout_bf, wdn_bf=wdn_bf, PS=PS, idb=idb, idf=idf, x_in=x_in, w_in_d=w_in_d, w_out_d=w_out_d, w_gu_d=w_gu_d,
                 w_dn_d=w_dn_d, g1_d=g1_d, g2_d=g2_d, og_d=og_d, qkg_d=qkg_d, cw_d=cw_d, cb_d=cb_d,
                 ta_d=ta_d, tabB_d=tabB_d, tabC_d=tabC_d, tm_d=tm_d, qkt_d=qkt_d, vs_d=vs_d,
                 mixT_d=mixT_d)
        for l in range(nlayers):
            x_src = x_in if l == 0 else xr_d
            x_dst = y_out if l == nlayers - 1 else xr_d
            if "a" in phases:
                phase_a(k, l, C, x_src, nta)
            if "t" in phases:
                phase_attn(k, l, C, mixers=("A", "B"))
            with ExitStack() as ws:
                k.open_scope()
                w_gu = None
                if "c" in phases:
                    w_gu = k.sb(ws, "w_gu_sb", [128, 8, 2 * DFF], BF16, multi=True)
                    for kc in range(8):
                        k.dma(w_gu[:, kc, :], w_gu_d[l, kc * 128:(kc + 1) * 128, :], q="pool")
                if "t" in phases:
                    phase_attn(k, l, C, mixers=("C",))
                if "c" in phases:
                    phase_cd(k, l, C, x_src, x_dst, w_gu)
                k.close_scope()
        k.finish()
    nc._kstats = {n: (len(e.insts), len(e.stamped)) for n, e in k.E.items()}
    return nc


def rmsnorm_tile(k, xin, g_bc, junk, ss, rs, hb):
    k.act(junk[:], xin, AF.Square)
    k.red(ss[:], junk[:])
    k.act(ss[:], ss[:], AF.Sqrt, bias=EPS, scale=1.0 / D)
    k.recip(rs[:], ss[:])
    k.stt("dve", hb[:], xin, rs[:, 0:1], g_bc[:], ALU.mult, ALU.mult)


def transpose8(k, PSb, idb, hb, dst_fn, eng="act"):
    pv = PSb[:].bitcast(BF16)
    for c in range(8):
        k.tr(pv[:, c * 128:(c + 1) * 128], hb[:, c * 128:(c + 1) * 128], idb[:])
    k.copy(eng, dst_fn(), pv[:, 0:1024].rearrange("p (c t) -> p c t", c=8))


def phase_a(k, l, C, x_src, nta=NT):
    PS, idb, w_in_d, qkt_d, vs_d = C["PS"], C["idb"], C["w_in_d"], C["qkt_d"], C["vs_d"]
    k.open_scope()
    with ExitStack() as st:
        tabB = k.sb(st, "tabB_sb", [128, NT, 2, 8], F32)
        tabC = k.sb(st, "tabC_sb", [128, NT, 2, 2, 16], F32)
        g1 = k.sb(st, "g1", [128, D], F32)
        qkg = k.sb(st, "qkg", [128, 1792], F32)
        k.dma(tabB[:], C["tabB_d"][:])
        k.dma(tabC[:], C["tabC_d"][:])
        k.dma(g1[:], C["g1_d"][l].partition_broadcast(128))
        k.dma(qkg[:], C["qkg_d"][l].partition_broadcast(128))
        w_in = k.sb(st, "w_in_sb", [128, 8, DIN], BF16, multi=True)
        for kc in range(8):
            if l == 0:
                k.dma(w_in[:, kc, :], w_in_d[l, kc * 128:(kc + 1) * 128, :], q="pool")
            else:
                k.dma(w_in[:, kc, :], C["win_bf"][l, kc * 128:(kc + 1) * 128, :])
        if l == 0:
            for l_ in range(C["nlayers"]):
                for r0 in range(0, D, 256):
                    k.dma(C["wout_bf"][l_, r0:r0 + 256, :], C["w_out_d"][l_, r0:r0 + 256, :], q="pool")
                for r0 in range(0, DFF, 256):
                    k.dma(C["wdn_bf"][l_, r0:r0 + 256, :], C["w_dn_d"][l_, r0:r0 + 256, :], q="pool")
                if l_ > 0:
                    for r0 in range(0, D, 128):
                        k.dma(C["win_bf"][l_, r0:r0 + 128, :], C["w_in_d"][l_, r0:r0 + 128, :], q="pool")
        xt = [k.sb(st, f"a_x{i}", [128, D], F32) for i in range(2)]
        junk = k.sb(st, "a_junk", [128, D], F32)
        ss = k.sb(st, "a_ss", [128, 1], F32)
        rs = k.sb(st, "a_rs", [128, 1], F32)
        hb = k.sb(st, "a_hb", [128, D], BF16)
        pr = [k.sb(st, f"a_pr{i}", [128, DIN], F32) for i in range(3)]
        sqb = k.sb(st, "a_sqb", [128, 1792], F32)
        ss28 = k.sb(st, "a_ss28", [128, 28], F32)
        rs28 = k.sb(st, "a_rs28", [128, 28], F32)
        rt = [k.sb(st, f"a_rt{i}", [128, 256], F32) for i in range(4)]
        qkb = [k.sb(st, f"a_qkb{i}", [128, 1920], BF16) for i in range(2)]
        vaug = [k.sb(st, f"a_vaug{i}", [128, 12, 65], BF16) for i in range(2)]
        qst = [k.sb(st, f"a_qst{i}", [128, 15, 512], BF16) for i in range(2)]
        for i in range(2):
            k.memset("pool", vaug[i][:, :, 64:65], 1.0)

        hT = [k.sb(st, f"a_hT3_{i}", [128, 8, 128], BF16) for i in range(3)]
        xt = xt + [k.sb(st, "a_x2", [128, D], F32)]

        def S1(t):
            x = xt[t % 3]
            k.dma(x[:], x_src[t * 128:(t + 1) * 128, :])
            rmsnorm_tile(k, x[:], g1, junk, ss, rs, hb)

        def S1t(t, c0, c1):
            pv = PS[5][:].bitcast(BF16)
            for c in range(c0, c1):
                k.tr(pv[:, c * 128:(c + 1) * 128], hb[:, c * 128:(c + 1) * 128], idb[:])
            if c1 == 8:
                k.copy("act", hT[t % 3][:], pv[:, 0:1024].rearrange("p (c t) -> p c t", c=8))

        def S2g(t, j):
            h = hT[t % 3]
            for kc in range(8):
                k.mm(PS[j][:, :], h[:, kc, :], w_in[:, kc, j * 512:(j + 1) * 512],
                     start=(kc == 0), stop=(kc == 7))

        def S2ev(t):
            p = pr[t % 3]
            for j in range(5):
                k.copy("act" if j % 2 == 0 else "dve", p[:, j * 512:(j + 1) * 512], PS[j][:, :])

        def S3(t):
            p = pr[t % 3]
            k.act(sqb[:], p[:, 0:1792], AF.Square)
            k.red(ss28[:], sqb[:].rearrange("p (h d) -> p h d", h=28))
            k.act(ss28[:], ss28[:], AF.Sqrt, bias=EPS, scale=1.0 / 64)
            k.recip(rs28[:], ss28[:])
            qv = p[:, 0:1792].rearrange("p (h d) -> p h d", h=28)
            k.tt("dve", qv, qv, rs28[:].unsqueeze(2).to_broadcast([128, 28, 64]), ALU.mult)

        def S3y(t):
            p = pr[t % 3]
            bv = p[:, 512:1280].rearrange("p (h d) -> p h d", h=12)
            gbv = qkg[:, 512:1280].rearrange("p (h d) -> p h d", h=12)
            k.tt("pool", bv[:, :, 0:16], bv[:, :, 0:16], gbv[:, :, 0:16], ALU.mult)
            k.tt("pool", p[:, 1280:1792], p[:, 1280:1792], qkg[:, 1280:1792], ALU.mult)
            x1, x2 = bv[:, :, 0:8], bv[:, :, 8:16]
            cB = tabB[:, t, 0, :].unsqueeze(1).to_broadcast([128, 12, 8])
            sB = tabB[:, t, 1, :].unsqueeze(1).to_broadcast([128, 12, 8])
            tv = [r[:, 0:96].rearrange("p (h d) -> p h d", h=12) for r in rt]
            k.tt("pool", tv[0], x1, cB, ALU.mult)
            k.tt("pool", tv[1], x2, sB, ALU.mult)
            k.tt("pool", tv[2], x2, cB, ALU.mult)
            k.tt("pool", tv[3], x1, sB, ALU.mult)
            k.tt("pool", x1, tv[0], tv[1], ALU.subtract)
            k.tt("pool", x2, tv[2], tv[3], ALU.add)
            cv = p[:, 1280:1792].rearrange("p (h a b d) -> p h a b d", h=8, a=2, b=2)
            y1, y2 = cv[:, :, :, 0, :], cv[:, :, :, 1, :]
            cC = tabC[:, t, 0, :, :].unsqueeze(1).to_broadcast([128, 8, 2, 16])
            sC = tabC[:, t, 1, :, :].unsqueeze(1).to_broadcast([128, 8, 2, 16])
            uv = [r[:, 0:256].rearrange("p (h a d) -> p h a d", h=8, a=2) for r in rt]
            k.tt("pool", uv[0], y1, cC, ALU.mult)
            k.tt("pool", uv[1], y2, sC, ALU.mult)
            k.tt("pool", uv[2], y2, cC, ALU.mult)
            k.tt("pool", uv[3], y1, sC, ALU.mult)
            k.tt("pool", y1, uv[0], uv[1], ALU.subtract)
            k.tt("pool", y2, uv[2], uv[3], ALU.add)
            qb = qkb[t % 2]
            k.tt("dve", qb[:, 0:512], p[:, 0:512], qkg[:, 0:512], ALU.mult)
            qbv = qb[:, 512:1280].rearrange("p (h d) -> p h d", h=12)
            k.tt("dve", qbv[:, :, 16:64], bv[:, :, 16:64], gbv[:, :, 16:64], ALU.mult)
            k.copy("pool", qbv[:, :, 0:16], bv[:, :, 0:16])
            k.copy("act", qb[:, 1280:1664], p[:, 1280:1664])
            k.copy("pool", qb[:, 1664:1920].rearrange("p (k u d) -> p k u d", k=2, u=2),
                   p[:, 1664:1792].rearrange("p (k u d) -> p k u d", k=2, u=1).to_broadcast([128, 2, 2, 64]))
            va = vaug[t % 2]
            k.copy("act", va[:, :, 0:64], p[:, 1792:2560].rearrange("p (h d) -> p h d", h=12))
            k.dma(vs_d[0][:, t, :, :], va[:, 0:4, :], q="act")
            k.dma(vs_d[1][:, t, :, :], va[:, 4:10, :], q="act")
            k.dma(vs_d[2][:, t, :, :], va[:, 10:12, :], q="act")

        def S3b(t, c0, c1):
            qb = qkb[t % 2]
            qs = qst[(t // 4) % 2]
            pv6 = PS[6][:].bitcast(BF16)
            pv7 = PS[7][:].bitcast(BF16)
            for c in range(c0, c1):
                dst = pv6[:, c * 128:(c + 1) * 128] if c < 8 else pv7[:, (c - 8) * 128:(c - 7) * 128]
                k.tr(dst, qb[:, c * 128:(c + 1) * 128], idb[:])
            if c1 < 15:
                return
            tl = (t % 4) * 128
            k.copy("dve", qs[:, 0:8, tl:tl + 128], pv6[:, 0:1024].rearrange("p (c t) -> p c t", c=8))
            k.copy("act", qs[:, 8:15, tl:tl + 128], pv7[:, 0:896].rearrange("p (c t) -> p c t", c=7))
            if t % 4 == 3:
                t0 = (t - 3) * 128
                k.dma(qkt_d[:, :, t0:t0 + 512].rearrange("c p t -> p c t"), qs[:], q="act")

        for i in range(-2, nta + 2):
            if 0 <= i < nta:
                S2ev(i)
            if 0 <= i + 2 < nta:
                S1(i + 2)
            if 0 <= i - 1 < nta:
                S3y(i - 1)
            if 0 <= i < nta:
                S3(i)
            a_ok, b_ok, c_ok = 0 <= i + 1 < nta, 0 <= i - 2 < nta, 0 <= i + 2 < nta
            if b_ok:
                S3b(i - 2, 0, 5)
            if a_ok:
                S2g(i + 1, 0)
            if b_ok:
                S3b(i - 2, 5, 10)
            if a_ok:
                S2g(i + 1, 1)
            if b_ok:
                S3b(i - 2, 10, 15)
            if a_ok:
                S2g(i + 1, 2)
            if c_ok:
                S1t(i + 2, 0, 4)
            if a_ok:
                S2g(i + 1, 3)
            if c_ok:
                S1t(i + 2, 4, 8)
            if a_ok:
                S2g(i + 1, 4)
        k.close_scope()


def phase_attn(k, l, C, mixers=("A", "B", "C")):
    PS, idb, idf, ta_d, qkt_d, vs_d, mixT_d = (C["PS"], C["idb"], C["idf"], C["ta_d"], C["qkt_d"],
                                                C["vs_d"], C["mixT_d"])
    for mixer in mixers:
        k.open_scope()
        with ExitStack() as st:
            og = k.sb(st, "og", [128, D], F32)
            if mixer == "B":
                tm = k.sb(st, "tm_sb", [128, 2944], BF16)
                k.dma(tm[:], C["tm_d"][:], q="pool")
            if mixer == "A":
                nh, qc0, vh0, nvh, mc0, oc0 = 4, 0, 0, 4, 0, 0
                chunks = [0, 1, 2, 3]
                kchunk = lambda h: 2 + h // 2
                vhead = lambda h: h
                segs = [(0, 256, list(range(0, 4)), "full"), (256, 256, list(range(0, 6)), "int")]
                for qb in range(1, 7):
                    segs.append((512 * qb, 512, list(range(4 * qb - 2, 4 * qb + 6)), "int"))
                segs.append((3584, 320, list(range(26, 32)), "int"))
                segs.append((3904, 192, list(range(28, 32)), "full"))
            elif mixer == "B":
                nh, qc0, vh0, nvh, mc0, oc0 = 6, 4, 4, 6, 2, 256
                chunks = [4, 5, 6, 7, 8, 9]
                kchunk = lambda h: 7 + h // 2
                vhead = lambda h: h
                segs = [(512 * qb, 512, list(range(max(0, 4 * qb - 8), min(32, 4 * qb + 12))), "band")
                        for qb in range(8)]
            else:
                nh, qc0, vh0, nvh, mc0, oc0 = 6, 10, 10, 2, 5, 640
                chunks = [10, 11, 12, 13, 14]
                kchunk = lambda h: 13 + h // 3
                vhead = lambda h: h // 3
                segs = [(512 * qb, 512, list(range(32)), None) for qb in range(8)]
            W = nh * 64
            nmc = W // 128
            qk = {}
            vsb = k.sb(st, "at_v", [128, NT, nvh, 65], BF16)
            first = True
            for c in chunks:
                if c >= qc0 + (nh + 1) // 2:
                    qk[c] = k.sb(st, f"at_qk{c}", [128, S], BF16)
                    k.dma(qk[c][:], qkt_d[c, :, :])
                    if first:
                        k.dma(vsb[:], vs_d["ABC".index(mixer)][:])
                        first = False
            qz = []
            for h in range(nh):
                t_ = k.sb(st, f"at_qz{h}", [128, S], BF16, multi=True)
                b_ = (h % 2) * 64
                k.memset("pool", t_[64 - b_:128 - b_, :], 0.0)
                k.dma(t_[b_:b_ + 64, :], qkt_d[qc0 + h // 2, b_:b_ + 64, :], q="act")
                qz.append(t_)
            k.dma(og[:], C["og_d"][l].partition_broadcast(128))
            oT = [k.sb(st, f"at_oT{i}", [65, 512], F32) for i in range(2)]
            rz = [k.sb(st, f"at_rz{i}", [128, 4], F32) for i in range(2)]
            att = [k.sb(st, f"at_att{i}", [128, 4, W], F32) for i in range(2)]
            junk = k.sb(st, "at_junk", [128, W], F32)
            ssn = k.sb(st, "at_ssn", [128, 4], F32)
            rsn = k.sb(st, "at_rsn", [128, 4], F32)
            mixb = k.sb(st, "at_mixb", [128, 4, W], BF16)
            mst = [k.sb(st, f"at_mst{i}", [128, nmc, 512], BF16) for i in range(2)]
            if mixer == "A":
                traw = k.sb(st, "at_traw", [128, 15, 64], F32, multi=True)
                etf = [k.sb(st, f"at_etf{h}", [128, 32, 64], BF16) for h in range(4)]
                eti = [k.sb(st, f"at_eti{h}", [128, 32, 64], BF16) for h in range(4)]
                for h in range(4):
                    src = ta_d[l, h].rearrange("i k c -> k i c")
                    k.dma(traw[0:64, :, :], src)
                    k.dma(traw[64:128, :, :], src)
                    k.memset("pool", etf[h][:], 0.0)
                    k.memset("pool", eti[h][:], 0.0)
                    k.act(etf[h][0:64, 8:23, :], traw[0:64, :, :], AF.Exp)
                    k.act(etf[h][64:128, 9:24, :], traw[64:128, :, :], AF.Exp)
                    k.copy("pool", eti[h][0:64, 12:20, :], etf[h][0:64, 12:20, :])
                    k.copy("pool", eti[h][64:128, 13:21, :], etf[h][64:128, 13:21, :])
            pT2 = []
            for i in range(3):
                hnd = st.enter_context(k.nc.sbuf_tensor(f"at_pTp{i}_{l}_{mixer}", [128, 2, 512], BF16))
                pT2.append((hnd, Tile(hnd[:, 0, :], f"pTa{i}"), Tile(hnd[:, 1, :], f"pTb{i}")))
            SP = [(PS[0], PS[1]), (PS[2], PS[3])]
            PO = [PS[4], PS[5]]
            items = []
            for si, (q0, nq, kts, mk) in enumerate(segs):
                for h in range(nh):
                    npair = len(kts) // 2
                    assert len(kts) % 2 == 0
                    for pi in range(npair):
                        items.append((si, h, pi, npair, kts[2 * pi], kts[2 * pi + 1]))
            deferred = []

            def issue_S(ii):
                si, h, pi, npair, ka, kb = items[ii]
                q0, nq, kts, mk = segs[si]
                kT = qk[kchunk(h)]
                for s_, kt in enumerate((ka, kb)):
                    k.mm(SP[ii % 2][s_][:, 0:nq], kT[:, kt * 128:(kt + 1) * 128], qz[h][:, q0:q0 + nq])

            junk4 = k.sb(st, "at_junk4", [128, 4, W], F32)

            def sched(due, fn):
                pos = len(deferred)
                while pos > 0 and deferred[pos - 1][0] > due:
                    pos -= 1
                deferred.insert(pos, (due, fn))

            def epi_head(ii, si, h, sc):
                q0, nq, kts, mk = segs[si]
                nsub = (nq + 127) // 128
                at = att[si % 2]
                ot = oT[sc % 2]
                rzz = rz[sc % 2]
                pt = PS[6]
                ptv = pt[:, 0:nsub * 65].rearrange("p (i e) -> p i e", e=65)

                def s1():
                    for i in range(nsub):
                        n = min(128, nq - i * 128)
                        k.tr(pt[0:n, i * 65:(i + 1) * 65], ot[0:65, i * 128:i * 128 + n], idf[0:65, 0:65])

                def s2():
                    k.recip(rzz[:, 0:nsub], ptv[:, :, 64])

                def s3():
                    k.tt("dve", at[:, 0:nsub, h * 64:(h + 1) * 64], ptv[:, :, 0:64],
                         rzz[:, 0:nsub].unsqueeze(2).to_broadcast([128, nsub, 64]), ALU.mult)
                sched(ii + 1, s1)
                sched(ii + 2, s2)
                sched(ii + 3, s3)

            def epi_seg(ii, si):
                q0, nq, kts, mk = segs[si]
                nsub = (nq + 127) // 128
                at = att[si % 2]
                ms = mst[si % 2]
                pm = PS[7][:].bitcast(BF16)
                atv = at[:, 0:nsub, :]

                def t0():
                    k.act(junk4[:, 0:nsub, :], atv, AF.Square)

                def t1_():
                    k.red(ssn[:, 0:nsub], junk4[:, 0:nsub, :])

                def t2():
                    k.act(ssn[:, 0:nsub], ssn[:, 0:nsub], AF.Sqrt, bias=EPS, scale=1.0 / W)

                def t3():
                    k.recip(rsn[:, 0:nsub], ssn[:, 0:nsub])

                def t4():
                    k.tt("dve", atv, atv, rsn[:, 0:nsub].unsqueeze(2).to_broadcast([128, nsub, W]), ALU.mult)

                def t5():
                    k.tt("dve", mixb[:, 0:nsub, :], atv,
                         og[:, oc0:oc0 + W].unsqueeze(1).to_broadcast([128, nsub, W]), ALU.mult)

                def trs(i0, i1):
                    def f():
                        for i in range(i0, i1):
                            n = min(128, nq - i * 128)
                            for c in range(nmc):
                                o_ = ((i - i0) * nmc + c) * 128
                                k.tr(pm[:, o_:o_ + n], mixb[0:n, i, c * 128:(c + 1) * 128], idb[0:n, 0:n])
                    return f

                def cps(i0, i1):
                    def f():
                        for i in range(i0, i1):
                            n = min(128, nq - i * 128)
                            o_ = (i - i0) * nmc * 128
                            k.copy("dve", ms[:, :, i * 128:i * 128 + n],
                                   pm[:, o_:o_ + nmc * 128].rearrange("p (c t) -> p c t", c=nmc)[:, :, 0:n])
                    return f

                def st_():
                    k.dma(mixT_d[mc0:mc0 + nmc, :, q0:q0 + nq].rearrange("c p t -> p c t"), ms[:, :, 0:nq],
                          q="pool")
                steps = [t0, t1_, t2, t3, t4, t5, trs(0, min(2, nsub)), cps(0, min(2, nsub))]
                if nsub > 2:
                    steps += [trs(2, nsub), cps(2, nsub)]
                steps.append(st_)
                for j, f in enumerate(steps):
                    sched(ii + 4 + j, f)

            issue_S(0)
            scnt = 0
            mcnt = 0
            for ii, (si, h, pi, npair, ka, kb) in enumerate(items):
                q0, nq, kts, mk = segs[si]
                if ii + 1 < len(items):
                    issue_S(ii + 1)
                phnd, pta, ptb = pT2[ii % 3]
                sa, sb_ = SP[ii % 2]
                spv = V(sa.pair[:, :].rearrange("p (b n) -> p b n", b=2)[:, :, 0:nq], sa)
                k.act(V(phnd[:, :, 0:nq], pta), spv, AF.Exp, scale=0.125, extra=[sb_], extra_w=[ptb])
                po = PO[scnt % 2]
                for s_, kt in enumerate((ka, kb)):
                    p = (pta, ptb)[s_]
                    meng = "dve" if (mcnt % 2 == 0 or mk == "band") else "pool"
                    mcnt += 1
                    if mk == "band":
                        u0 = 1408 - (kt * 128 - q0)
                        k.tt(meng, p[:, 0:nq], p[:, 0:nq], tm[:, u0:u0 + nq], ALU.mult)
                    elif mk in ("full", "int"):
                        tab = etf[h] if mk == "full" else eti[h]
                        rq0 = q0 // 64
                        nr = nq // 64
                        s0 = 7 - 2 * kt + rq0 + 8
                        assert 0 <= s0 and s0 + nr <= 32, (s0, nr, kt, q0)
                        pv = p[:, 0:nq].rearrange("p (j c) -> p j c", c=64)
                        k.tt(meng, pv, pv, tab[:, s0:s0 + nr, :], ALU.mult)
                    k.mm(po[0:65, 0:nq], vsb[:, kt, vhead(h), :], p[:, 0:nq],
                         start=(pi == 0 and s_ == 0), stop=(pi == npair - 1 and s_ == 1))
                if pi == npair - 1:
                    k.copy("dve", oT[scnt % 2][:, 0:nq], po[0:65, 0:nq])
                    epi_head(ii, si, h, scnt)
                    if h == nh - 1:
                        epi_seg(ii, si)
                    scnt += 1
                while deferred and deferred[0][0] <= ii:
                    deferred.pop(0)[1]()
            while deferred:
                deferred.pop(0)[1]()
            k.close_scope()


def phase_cd(k, l, C, x_src, x_dst, w_gu):
    PS, idb, w_out_d, w_gu_d, w_dn_d, mixT_d = (C["PS"], C["idb"], C["w_out_d"], C["w_gu_d"], C["w_dn_d"],
                                                 C["mixT_d"])
    k.open_scope()
    with ExitStack() as st:
        g2 = k.sb(st, "g2", [128, D], F32)
        cw = k.sb(st, "cw", [128, NCH, 3], F32)
        cb = k.sb(st, "cb", [128, NCH], F32)
        k.dma(g2[:], C["g2_d"][l].partition_broadcast(128))
        k.dma(cw[:], C["cw_d"][l])
        k.dma(cb[:], C["cb_d"][l])
        w_out = k.sb(st, "w_out_sb", [128, 8, D], BF16, multi=True)
        w_dn = k.sb(st, "w_dn_sb", [128, NCH, D], BF16, multi=True)
        for kc in range(8):
            k.dma(w_out[:, kc, :], C["wout_bf"][l, kc * 128:(kc + 1) * 128, :])
        for cc in range(NCH):
            k.dma(w_dn[:, cc, :], C["wdn_bf"][l, cc * 128:(cc + 1) * 128, :])
        mT = [k.sb(st, f"c_mT{i}", [128, 8, 128], BF16) for i in range(2)]
        x1 = [k.sb(st, f"c_x1_{i}", [128, D], F32) for i in range(6)]
        junk = k.sb(st, "c_junk", [128, D], F32)
        ss = k.sb(st, "c_ss", [128, 1], F32)
        rs = k.sb(st, "c_rs", [128, 1], F32)
        hb = k.sb(st, "c_hb", [128, D], BF16)
        h2T = [k.sb(st, f"c_h2T{i}", [128, 8, TB + 2], BF16) for i in range(3)]
        t1 = [k.sb(st, f"c_t1_{i}", [128, TB], F32) for i in range(2)]
        ge = [k.sb(st, f"c_ge{i}", [128, TB], F32) for i in range(2)]
        mm_ = [k.sb(st, f"c_m{i}", [128, TB], BF16) for i in range(2)]

        def stage1a(b, tt, nb, ps):
            t = b * 2 + tt
            xx = x1[(b % 3) * 2 + tt]
            m = mT[t % 2]
            if nb == 0:
                k.dma(xx[:], x_src[t * 128:(t + 1) * 128, :])
                k.dma(m[:], mixT_d[:, :, t * 128:(t + 1) * 128].rearrange("c p t -> p c t"))
            for kc in range(8):
                k.mm(ps[:, :], m[:, kc, :], w_out[:, kc, nb * 512:(nb + 1) * 512],
                     start=(kc == 0), stop=(kc == 7))
            k.tt("dve", xx[:, nb * 512:(nb + 1) * 512], ps[:, :], xx[:, nb * 512:(nb + 1) * 512], ALU.add)
            if nb == 1:
                rmsnorm_tile(k, xx[:], g2, junk, ss, rs, hb)

        def stage1b(b, tt, pst, halo):
            hT = h2T[b % 3]
            transpose8(k, pst, idb, hb, lambda: hT[:, :, 1 + tt * 128:1 + (tt + 1) * 128], eng="act")
            if not halo:
                return
            if b == 0:
                k.memset("pool", hT[:, :, 0:1], 0.0)
            else:
                hp = h2T[(b - 1) % 3]
                k.copy("pool", hT[:, :, 0:1], hp[:, :, TB:TB + 1])
                k.copy("pool", hp[:, :, TB + 1:TB + 2], hT[:, :, 1:2])
            if b == NB - 1:
                k.memset("pool", hT[:, :, TB + 1:TB + 2], 0.0)

        def stage1(b):
            for tt in range(2):
                stage1a(b, tt, 0, PS[4])
                stage1a(b, tt, 1, PS[5])
                stage1b(b, tt, PS[6], tt == 1)

        PG = [PS[0], PS[1]]
        PU = [PS[2], PS[3]]

        def stage2(b, mid=None):
            hT = h2T[b % 3]

            def g_mm(cc):
                pg = PG[cc % 2]
                for kc in range(8):
                    k.mm(pg[:, 0:TB + 2], w_gu[:, kc, cc * 128:(cc + 1) * 128], hT[:, kc, 0:TB + 2],
                         start=(kc == 0), stop=(kc == 7))

            def u_mm(cc):
                pu = PU[cc % 2]
                for kc in range(8):
                    k.mm(pu[:, 0:TB], w_gu[:, kc, DFF + cc * 128:DFF + (cc + 1) * 128], hT[:, kc, 1:TB + 1],
                         start=(kc == 0), stop=(kc == 7))

            def conv(cc):
                pg = PG[cc % 2]
                a = t1[cc % 2]
                k.act(a[:], pg[:, 1:TB + 1], AF.Identity, bias=cb[:, cc:cc + 1], scale=cw[:, cc, 1:2])
                k.stt("dve", a[:], pg[:, 0:TB], cw[:, cc, 0:1], a[:], ALU.mult, ALU.add)
                k.stt("dve", a[:], pg[:, 2:TB + 2], cw[:, cc, 2:3], a[:], ALU.mult, ALU.add)
                k.act(ge[cc % 2][:], a[:], AF.Gelu)

            def mult(cc):
                k.tt("dve", mm_[cc % 2][:], ge[cc % 2][:], PU[cc % 2][:, 0:TB], ALU.mult)

            def down(cc):
                m = mm_[cc % 2]
                for tt in range(2):
                    for nb in range(2):
                        k.mm(PS[4 + tt * 2 + nb][:, :], m[:, tt * 128:(tt + 1) * 128],
                             w_dn[:, cc, nb * 512:(nb + 1) * 512], start=(cc == 0), stop=(cc == NCH - 1))

            g_mm(0)
            g_mm(1)
            u_mm(0)
            conv(0)
            mult(0)
            for cc in range(NCH):
                if cc + 2 < NCH:
                    g_mm(cc + 2)
                if cc + 1 < NCH:
                    u_mm(cc + 1)
                    conv(cc + 1)
                    mult(cc + 1)
                down(cc)
                if mid is not None and cc in mid:
                    mid[cc](PU[cc % 2])
            for tt in range(2):
                t = b * 2 + tt
                xx = x1[(b % 3) * 2 + tt]
                for nb in range(2):
                    k.tt("dve", xx[:, nb * 512:(nb + 1) * 512], PS[4 + tt * 2 + nb][:, :],
                         xx[:, nb * 512:(nb + 1) * 512], ALU.add)
                k.dma(x_dst[t * 128:(t + 1) * 128, :], xx[:], q="pool")

        stage1(0)
        stage1(1)
        for b in range(NB):
            mid = None
            if b + 2 < NB:
                mid = {2: (lambda ps, b=b: stage1a(b + 2, 0, 0, ps)),
                       4: (lambda ps, b=b: stage1a(b + 2, 0, 1, ps)),
                       7: (lambda ps, b=b: stage1b(b + 2, 0, ps, False)),
                       10: (lambda ps, b=b: stage1a(b + 2, 1, 0, ps)),
                       12: (lambda ps, b=b: stage1a(b + 2, 1, 1, ps)),
                       15: (lambda ps, b=b: stage1b(b + 2, 1, ps, True))}
            stage2(b, mid)
    k.close_scope()


def _rope(pos, dim, theta):
    inv = (np.float32(theta) ** (-np.arange(0, dim, 2, dtype=np.float32) / np.float32(dim))).astype(np.float32)
    ang = pos.astype(np.float32)[:, None] * inv[None, :]
    return np.cos(ang).astype(np.float32), np.sin(ang).astype(np.float32)


def _consts():
    t = np.arange(S)
    c1, s1 = _rope(t, 16, 500000.0)
    cr, sr = _rope(t // 64, 32, 10000.0)
    cc, sc = _rope(t % 64, 32, 10000.0)
    tabB = np.stack([c1, s1], axis=1)
    tabB = tabB.reshape(NT, 128, 2, 8).transpose(1, 0, 2, 3)
    tabC = np.stack([np.stack([cr, cc], axis=1), np.stack([sr, sc], axis=1)], axis=1)
    tabC = tabC.reshape(NT, 128, 2, 2, 16).transpose(1, 0, 2, 3, 4)
    kl = np.arange(128)[:, None]
    u = np.arange(2944)[None, :]
    dlt = kl - u + 1408
    a = np.abs(dlt)
    m = (a <= 64).astype(np.float32) + ((a <= 256) & (dlt % 4 == 0)) + ((a <= 1024) & (dlt % 16 == 0))
    return (np.ascontiguousarray(tabB, dtype=np.float32), np.ascontiguousarray(tabC, dtype=np.float32),
            np.ascontiguousarray(m, dtype=np.float32))


def _prep(inp):
    f = lambda a: np.ascontiguousarray(np.asarray(a), dtype=np.float32)
    perm = np.concatenate([np.arange(0, 256), np.arange(256, 512), np.arange(768, 1152), np.arange(1152, 1536),
                           np.arange(1920, 2304), np.arange(2304, 2432), np.arange(512, 768),
                           np.arange(1536, 1920), np.arange(2432, 2560)])
    qg, kg = f(inp["q_norm_g"]), f(inp["k_norm_g"])
    qkg = np.stack([np.concatenate([np.tile(qg[l, 0], 4), np.tile(kg[l, 0], 4), np.tile(qg[l, 1], 6),
                                    np.tile(kg[l, 1], 6), np.tile(qg[l, 2], 6), np.tile(kg[l, 2], 2)])
                    for l in range(NL)])
    cwv = f(inp["conv_w"])
    cw = cwv.transpose(0, 2, 1).reshape(NL, NCH, 128, 3).transpose(0, 2, 1, 3)
    cb = f(inp["conv_b"]).reshape(NL, NCH, 128).transpose(0, 2, 1)
    rpb = f(inp["rpb"])
    rp = np.concatenate([rpb.reshape(NL, 4, 15 * 31), np.full((NL, 4, 1), NEG, np.float32)], axis=2)
    idx = np.arange(15)[:, None, None]
    kc = np.arange(64)[None, :, None]
    c = np.arange(64)[None, None, :]
    c0 = np.clip(c - 8, 0, 48)
    valid = (kc >= c0) & (kc < c0 + 16)
    flat = (14 - idx) * 31 + np.clip(kc - c + 15, 0, 30)
    flat = np.where(valid, flat, 15 * 31)
    ta = rp[:, :, flat]
    tabB, tabC, tm = _consts()
    shared = {
        "w_in": np.ascontiguousarray(f(inp["w_in"])[:, :, perm]),
        "w_out": f(inp["w_out"]), "w_gu": f(inp["w_gate_up"]), "w_dn": f(inp["w_down"]),
        "g1": f(inp["norm1_g"]), "g2": f(inp["norm2_g"]), "og": f(inp["out_norm_g"]),
        "qkg": np.ascontiguousarray(qkg, dtype=np.float32),
        "cw": np.ascontiguousarray(cw), "cb": np.ascontiguousarray(cb),
        "ta": np.ascontiguousarray(ta, dtype=np.float32),
        "tabB": tabB, "tabC": tabC, "tm": tm,
    }
    return shared


def kernel(**inputs):
    x = np.asarray(inputs["x"], dtype=np.float32)
    shared = _prep(inputs)
    nc = build()
    in_maps = [dict(shared, x=np.ascontiguousarray(x[b])) for b in range(8)]
    res = run_bass_kernel_spmd(nc, in_maps, core_ids=list(range(8)))
    return np.stack([np.asarray(r["y"], dtype=np.float32) for r in res.results], axis=0)
```
